# Optimizing a Trainium2 kernel written in Bass

```python
import jax, jax.numpy as jnp
from jax import lax
import numpy as np

D_MODEL = 2048
BATCH = 1
SEQ = 8192
DEPTH = 4

EPS = 1e-6
D_A = D_MODEL // 2
CONV_WIDTH = 31
D_B = D_MODEL // 2
CHUNK = 128
HEAD_DIM_B = 128
N_HEADS_B = D_B // HEAD_DIM_B
D_IN_AB = 2 * D_A + 2 * D_B
POOL_WINDOWS = (2, 4, 8, 16)
N_POOL_GROUPS = len(POOL_WINDOWS)
D_POOL_GROUP = D_MODEL // N_POOL_GROUPS
D_FF = -(-8 * D_MODEL // (3 * 256)) * 256
N_MOD = 6
N_EVEN = (DEPTH + 1) // 2
N_ODD = DEPTH // 2

kernel_name = "hybrid_conv_gmlp_pool_adaln_trunk"


def rmsnorm(x, g):
    xf = x.astype(jnp.float32)
    y = xf * lax.rsqrt(jnp.mean(xf * xf, axis=-1, keepdims=True) + EPS)
    return (y * g.astype(jnp.float32)).astype(x.dtype)


def layernorm(x, g, b):
    xf = x.astype(jnp.float32)
    mu = jnp.mean(xf, axis=-1, keepdims=True)
    xc = xf - mu
    y = xc * lax.rsqrt(jnp.mean(xc * xc, axis=-1, keepdims=True) + EPS)
    return (y * g.astype(jnp.float32) + b.astype(jnp.float32)).astype(x.dtype)


def modulate(h, shift, scale):
    return h * (1.0 + scale[:, None, :]) + shift[:, None, :]


def mixer_ab(h, w_in, conv_w, conv_b, a_g, a_b, v_g, v_b, w_s, s_bias, w_out):
    bsz, s, _ = h.shape
    proj = h @ w_in
    a_val = proj[..., :D_A]
    a_gate = proj[..., D_A:2 * D_A]
    b_u = proj[..., 2 * D_A:2 * D_A + D_B]
    b_v = proj[..., 2 * D_A + D_B:]

    a = a_val * jax.nn.sigmoid(a_gate)
    a = lax.conv_general_dilated(
        a, conv_w[:, None, :].astype(a.dtype), window_strides=(1,),
        padding=[(CONV_WIDTH - 1, 0)],
        dimension_numbers=('NWC', 'WIO', 'NWC'),
        feature_group_count=D_A) + conv_b
    a = jax.nn.silu(layernorm(a, a_g, a_b))

    v = layernorm(b_v, v_g, v_b).reshape(bsz, s // CHUNK, CHUNK, N_HEADS_B, HEAD_DIM_B)
    causal = jnp.tril(jnp.ones((CHUNK, CHUNK), dtype=bool))
    w_c = jnp.where(causal[None], w_s, jnp.zeros_like(w_s))
    v = jnp.einsum('hts,bnshd->bnthd', w_c, v) + s_bias.T[:, :, None]
    b_out = b_u * v.reshape(bsz, s, D_B)

    return jnp.concatenate([a, b_out], axis=-1) @ w_out


def mixer_c(h, pool_w, pool_scale):
    bsz, s, _ = h.shape
    hf = h.astype(jnp.float32)
    cs = jnp.cumsum(hf, axis=1)
    count_base = (jnp.arange(s) + 1).astype(jnp.float32)
    pooled = []
    for gi, w in enumerate(POOL_WINDOWS):
        sl = slice(gi * D_POOL_GROUP, (gi + 1) * D_POOL_GROUP)
        cg = cs[..., sl]
        lagged = jnp.pad(cg[:, :s - w], ((0, 0), (w, 0), (0, 0)))
        cnt = jnp.minimum(count_base, float(w))[None, :, None]
        pooled.append(((cg - lagged) / cnt - hf[..., sl]).astype(h.dtype))
    p = jnp.stack(pooled, axis=2)
    y = jnp.einsum('bsgi,gio->bsgo', p, pool_w).reshape(bsz, s, D_MODEL)
    return y * pool_scale


def swiglu(h, w1, w3, w2):
    return (jax.nn.silu(h @ w1) * (h @ w3)) @ w2


def setup_inputs(seed: int = 0) -> dict:
    key = jax.random.key(seed)
    ks = jax.random.split(key, 24)
    f32 = jnp.float32
    nrm = lambda k, shape, s: (jax.random.normal(k, shape, f32) * s)
    gain = lambda k, shape: 1.0 + 0.02 * jax.random.normal(k, shape, f32)
    return {
        "x": nrm(ks[0], (BATCH, SEQ, D_MODEL), 1.0),
        "c": nrm(ks[1], (BATCH, D_MODEL), 1.0),
        "ada_w": nrm(ks[2], (DEPTH, D_MODEL, N_MOD * D_MODEL), 0.5 * D_MODEL ** -0.5),
        "ada_b": nrm(ks[3], (DEPTH, N_MOD * D_MODEL), 0.02),
        "norm_mix_g": gain(ks[4], (DEPTH, D_MODEL)),
        "norm_ffn_g": gain(ks[5], (DEPTH, D_MODEL)),
        "ab_w_in": nrm(ks[6], (N_EVEN, D_MODEL, D_IN_AB), D_MODEL ** -0.5),
        "a_conv_w": nrm(ks[7], (N_EVEN, CONV_WIDTH, D_A), CONV_WIDTH ** -0.5),
        "a_conv_b": nrm(ks[8], (N_EVEN, D_A), 0.02),
        "a_norm_g": gain(ks[9], (N_EVEN, D_A)),
        "a_norm_b": nrm(ks[10], (N_EVEN, D_A), 0.02),
        "b_norm_g": gain(ks[11], (N_EVEN, D_B)),
        "b_norm_b": nrm(ks[12], (N_EVEN, D_B), 0.02),
        "b_w_s": nrm(ks[13], (N_EVEN, N_HEADS_B, CHUNK, CHUNK), CHUNK ** -0.5),
        "b_bias": gain(ks[14], (N_EVEN, N_HEADS_B, CHUNK)),
        "ab_w_out": nrm(ks[15], (N_EVEN, D_A + D_B, D_MODEL), (D_A + D_B) ** -0.5),
        "pool_w": nrm(ks[16], (N_ODD, N_POOL_GROUPS, D_POOL_GROUP, D_POOL_GROUP), D_POOL_GROUP ** -0.5),
        "pool_scale": 1.0 + 0.1 * jax.random.normal(ks[17], (N_ODD, D_MODEL), f32),
        "ffn_w1": nrm(ks[18], (DEPTH, D_MODEL, D_FF), D_MODEL ** -0.5),
        "ffn_w3": nrm(ks[19], (DEPTH, D_MODEL, D_FF), D_MODEL ** -0.5),
        "ffn_w2": nrm(ks[20], (DEPTH, D_FF, D_MODEL), D_FF ** -0.5),
        "final_g": gain(ks[21], (D_MODEL,)),
    }


def reference(x, c, ada_w, ada_b, norm_mix_g, norm_ffn_g, ab_w_in, a_conv_w, a_conv_b,
              a_norm_g, a_norm_b, b_norm_g, b_norm_b, b_w_s, b_bias, ab_w_out,
              pool_w, pool_scale, ffn_w1, ffn_w3, ffn_w2, final_g):
    cond = jax.nn.silu(c)
    for l in range(DEPTH):
        mod = cond @ ada_w[l] + ada_b[l]
        sh1, sc1, g1, sh2, sc2, g2 = jnp.split(mod, N_MOD, axis=-1)
        h = modulate(rmsnorm(x, norm_mix_g[l]), sh1, sc1)
        i = l // 2
        if l % 2 == 0:
            y = mixer_ab(h, ab_w_in[i], a_conv_w[i], a_conv_b[i], a_norm_g[i], a_norm_b[i],
                         b_norm_g[i], b_norm_b[i], b_w_s[i], b_bias[i], ab_w_out[i])
        else:
            y = mixer_c(h, pool_w[i], pool_scale[i])
        x = x + g1[:, None, :] * y
        h = modulate(rmsnorm(x, norm_ffn_g[l]), sh2, sc2)
        x = x + g2[:, None, :] * swiglu(h, ffn_w1[l], ffn_w3[l], ffn_w2[l])
    return rmsnorm(x, final_g)
```

```python
import numpy as np
import concourse.bass as bass
import concourse.mybir as mybir
from concourse.bass_utils import run_bass_kernel_spmd

F32 = mybir.dt.float32
BF16 = mybir.dt.bfloat16
AF = mybir.ActivationFunctionType
ALU = mybir.AluOpType
AX = mybir.AxisListType

D = 2048
KC = 16
NCORE = 8
OWN = 1024
HALO = 256
NT = OWN + HALO
XO = 112
NX = NT - XO
DFF = 5632
JC = DFF // 128
GS = 4
NG = JC // GS
DEPTH = 4
EPS = 1e-6
NSLOT = 8
WAIT_SLACK = 3
UNIT = 2048
CONVW = 31
POOLW = (2, 4, 8, 16)

MIX_H_START = [0, 112, 128, 240]
MIX_OUT_START = [112, 128, 240, 256]
A_BACK = 32

_off = 0
def _take(n):
    global _off
    o = _off
    _off += n
    return o
S_ADAB = _take(DEPTH * 96)
S_GMIX = _take(DEPTH * 16)
S_GFFN = _take(DEPTH * 16)
S_FING = _take(16)
S_CONVW = _take(2 * 8 * CONVW)
S_CONVB = _take(2 * 8)
S_AG = _take(2 * 8)
S_AB = _take(2 * 8)
S_VG = _take(2 * 8)
S_VB = _take(2 * 8)
S_PSC = _take(2 * 16)
S_CVEC = _take(16)
NS = _off


def tiles(s, e, maxt=512):
    n = -(-(e - s) // maxt)
    base = -(-(e - s) // n)
    base = -(-base // 8) * 8
    out = []
    t = s
    while t < e:
        T = min(base, e - t)
        out.append((t, T))
        t += T
    return out


NORM_FILL = 15


def ada_ffn_base(l):
    return 2 * NORM_FILL if l == 2 else (NORM_FILL if l == 1 else 0)


def ada_split(step, first=0, nsteps=JC - GS):
    step -= GS
    if step < 0:
        return 0, 0
    n = 96 - first
    lo = first + -(-n * step // nsteps)
    hi = first + -(-n * (step + 1) // nsteps)
    return lo, hi


def unit_plan():
    plan = []
    for c in range(32):
        plan.append(("ada", 0, c))
    for l in range(DEPTH):
        i = l // 2
        if l % 2 == 0:
            if l == 2:
                for c in range(NORM_FILL):
                    plan.append(("ada", 3, c))
            for c in range(8):
                plan.append(("win", i, c))
                plan.append(("win", i, 8 + c))
                if l == 0:
                    for cc in range(32 + 8 * c, 40 + 8 * c):
                        plan.append(("ada", 0, cc))
            if l == 2:
                for c in range(NORM_FILL, 2 * NORM_FILL):
                    plan.append(("ada", 3, c))
            for fp in range(8):
                plan.append(("wout", i, 0, fp))
            for c in range(8):
                plan.append(("win", i, 16 + c))
            for half in range(2):
                for kg in range(4):
                    plan.append(("winv", i, half, kg))
            for fp in range(8):
                plan.append(("wout", i, 1, fp))
        else:
            if l + 1 < DEPTH:
                for c in range(NORM_FILL):
                    plan.append(("ada", l + 1, c))
            for g in range(4):
                plan.append(("pool", i, g))
        if l + 1 < DEPTH:
            for c in range(ada_ffn_base(l), ada_ffn_base(l) + NORM_FILL):
                plan.append(("ada", l + 1, c))
        for q in range(NG):
            for jj in range(GS):
                plan.append(("w1", l, q * GS + jj))
                plan.append(("w3", l, q * GS + jj))
                if l + 1 < DEPTH:
                    lo, hi = ada_split(q * GS + jj, ada_ffn_base(l) + NORM_FILL)
                    for c in range(lo, hi):
                        plan.append(("ada", l + 1, c))
            for jj in range(GS):
                plan.append(("w2", l, q * GS + jj))
    return plan


def _kmajor(a):
    return a.reshape(16, 128, 128).transpose(1, 0, 2).reshape(128, UNIT)


def fill_unit(spec, inp, out):
    kind = spec[0]
    if kind == "ada":
        _, l, c = spec
        out[...] = _kmajor(inp["ada_w"][l][:, c * 128:(c + 1) * 128])
    elif kind == "win":
        _, i, m = spec
        out[...] = _kmajor(inp["ab_w_in"][i][:, m * 128:(m + 1) * 128])
    elif kind == "winv":
        _, i, half, kg = spec
        w = inp["ab_w_in"][i][kg * 512:(kg + 1) * 512, 3072 + half * 512:3072 + (half + 1) * 512]
        out[...] = w.reshape(4, 128, 512).transpose(1, 0, 2).reshape(128, UNIT)
    elif kind == "wout":
        _, i, part, fp = spec
        w = inp["ab_w_out"][i][part * 1024:(part + 1) * 1024, fp * 256:(fp + 1) * 256]
        out[...] = w.reshape(8, 128, 2, 128).transpose(1, 2, 0, 3).reshape(128, UNIT)
    elif kind == "w1":
        _, l, j = spec
        out[...] = _kmajor(inp["ffn_w1"][l][:, j * 128:(j + 1) * 128])
    elif kind == "w3":
        _, l, j = spec
        out[...] = _kmajor(inp["ffn_w3"][l][:, j * 128:(j + 1) * 128])
    elif kind == "w2":
        _, l, j = spec
        out[...] = inp["ffn_w2"][l][j * 128:(j + 1) * 128, :]
    elif kind == "pool":
        _, i, g = spec
        out[...] = inp["pool_w"][i][g].reshape(4, 128, 512).transpose(1, 0, 2).reshape(128, UNIT)
    else:
        raise ValueError(spec)


class Tok:
    __slots__ = ("eng", "sem", "val", "key")

    def __init__(self, eng, sem, val, key):
        self.eng, self.sem, self.val, self.key = eng, sem, val, key


class Eng:
    def __init__(self, nc, raw, name):
        self.nc, self.raw, self.name = nc, raw, name
        self.sem = nc.alloc_semaphore("es_" + name)
        self.count = 0
        self.waited = {}

    def wait(self, tok):
        if tok is None or (tok.eng is self and self.name == "pe"):
            return
        if self.waited.get(tok.key, 0) >= tok.val:
            return
        val = tok.val
        if tok.eng is not None and tok.eng is not self:
            alt = tok.eng.count - WAIT_SLACK
            if alt > val:
                val = alt
        self.raw.wait_ge(tok.sem, val)
        self.waited[tok.key] = val

    def signal(self, ins):
        ins.then_inc(self.sem, 1)
        self.count += 1
        return Tok(self, self.sem, self.count, self.name)


class Buf:
    def __init__(self, name, nc=None, dma=False):
        self.name = name
        self.w = None
        self.r = {}
        if dma:
            self.dsem = nc.alloc_semaphore("ds_" + name)
            self.dcount = 0

    def deps(self, write, lo=None, hi=None):
        out = [self.w]
        if write:
            out.extend(self.r.values())
        return out

    def record(self, tok, write, ename, lo=None, hi=None):
        if write:
            self.w = tok
            self.r = {}
        else:
            self.r[ename] = tok


class RBuf:
    def __init__(self, name):
        self.name = name
        self.ent = []

    def deps(self, write, lo, hi):
        out = []
        for e in self.ent:
            if e[0] < hi and lo < e[1] and (write or e[3]):
                out.append(e[2])
        return out

    def record(self, tok, write, ename, lo, hi):
        if write:
            self.ent = [e for e in self.ent if not (lo <= e[0] and e[1] <= hi)]
        else:
            self.ent = [e for e in self.ent if not ((not e[3]) and e[4] == ename and lo <= e[0] and e[1] <= hi)]
        self.ent.append([lo, hi, tok, write, ename])

    def set_all(self, tok, lo, hi):
        self.ent = [[lo, hi, tok, True, "init"]]


def _items(lst):
    for it in lst:
        if isinstance(it, tuple):
            yield it
        else:
            yield (it, None, None)


def op(E, fn, reads=(), writes=()):
    rl = list(_items(reads))
    wl = list(_items(writes))
    for b, lo, hi in rl:
        for t in b.deps(False, lo, hi):
            E.wait(t)
    for b, lo, hi in wl:
        for t in b.deps(True, lo, hi):
            E.wait(t)
    ins = fn()
    tok = E.signal(ins)
    for b, lo, hi in rl:
        b.record(tok, False, E.name, lo, hi)
    for b, lo, hi in wl:
        b.record(tok, True, E.name, lo, hi)
    return ins


def dma_load(Q, buf, out_ap, in_ap, reads=()):
    rl = list(_items(reads))
    for b, lo, hi in rl:
        for t in b.deps(False, lo, hi):
            Q.wait(t)
    if buf.w is not None and buf.w.key != "d_" + buf.name:
        Q.wait(buf.w)
    for t in buf.r.values():
        Q.wait(t)
    ins = Q.raw.dma_start(out=out_ap, in_=in_ap)
    ins.then_inc(buf.dsem, 16)
    buf.dcount += 16
    tok = Tok(None, buf.dsem, buf.dcount, "d_" + buf.name)
    buf.w = tok
    buf.r = {}
    for b, lo, hi in rl:
        b.record(tok, False, "d_" + buf.name, lo, hi)
    return tok


def build_program(n_layers=DEPTH, dbg=False, plan_len=None, stage=99):
    stage_in = stage
    nc = bass.Bass("TRN2", target_bir_lowering=False)
    plan = unit_plan()[:plan_len]
    NU = len(plan)

    d_x = nc.dram_tensor("xT", [16, 128, NT], F32, kind="ExternalInput").ap()
    d_w = nc.dram_tensor("wstream", [NU, 128, UNIT], F32, kind="ExternalInput").ap()
    d_sm = nc.dram_tensor("smalls", [128, NS], F32, kind="ExternalInput").ap()
    d_hmask = nc.dram_tensor("hmask", [128, 256], F32, kind="ExternalInput").ap()
    d_invc = nc.dram_tensor("invcnt", [128, 64], F32, kind="ExternalInput").ap()
    d_ident = nc.dram_tensor("ident", [128, 128], F32, kind="ExternalInput").ap()
    d_cmask = nc.dram_tensor("cmask", [128, 128], F32, kind="ExternalInput").ap()
    d_bbias = nc.dram_tensor("bbias", [2, 128, 1024], F32, kind="ExternalInput").ap()
    d_wsT = nc.dram_tensor("wsT", [2, 128, 1024], F32, kind="ExternalInput").ap()
    d_out = nc.dram_tensor("outT", [16, 128, OWN], F32, kind="ExternalOutput").ap()
    if dbg:
        d_dbg = nc.dram_tensor("dbg", [16, 128, NX], F32, kind="ExternalOutput").ap()

    x_t = nc.alloc_sbuf_tensor("x_t", [128, 16, NX], F32)
    hb_t = nc.alloc_sbuf_tensor("hb_t", [128, 16, NT], BF16)
    ring_t = nc.alloc_sbuf_tensor("ring_t", [128, NSLOT, UNIT], BF16)
    SCR_BYTES = 52224
    scr_t = nc.alloc_sbuf_tensor("scr_t", [128, SCR_BYTES // 2], BF16)
    sm_t = nc.alloc_sbuf_tensor("sm_t", [128, NS], F32)
    hmask_t = nc.alloc_sbuf_tensor("hmask_t", [128, 256], F32)
    invc_t = nc.alloc_sbuf_tensor("invc_t", [128, 64], F32)
    ident_t = nc.alloc_sbuf_tensor("ident_t", [128, 128], F32)
    cmask_t = nc.alloc_sbuf_tensor("cmask_t", [128, 128], F32)
    ones_t = nc.alloc_sbuf_tensor("ones_t", [128, 128], BF16)
    identb_t = nc.alloc_sbuf_tensor("identb_t", [128, 128], BF16)
    cond_t = nc.alloc_sbuf_tensor("cond_t", [128, 16], BF16)
    mod_t = nc.alloc_sbuf_tensor("mod_t", [128, 2, 96], F32)
    vec_t = nc.alloc_sbuf_tensor("vec_t", [128, 2, 4, 16], F32)
    st_t = nc.alloc_sbuf_tensor("st_t", [128, 16], F32)
    eps_t = nc.alloc_sbuf_tensor("eps_t", [128, 1], F32)

    scr = scr_t[:]

    def carve(off_bytes, nbytes, dtype, shape):
        a = scr[:, off_bytes // 2:(off_bytes + nbytes) // 2]
        if dtype == F32:
            a = a.bitcast(F32)
        if shape is None:
            return a
        if len(shape) == 2:
            return a.rearrange("p (a b) -> p a b", a=shape[0])
        if len(shape) == 3:
            return a.rearrange("p (a b c) -> p a b c", a=shape[0], b=shape[1])
        return a

    o = 0
    CAT_B = 8 * NX * 2
    cat = carve(o, CAT_B, BF16, (8, NX)); o += CAT_B
    gbuf = carve(0, GS * NX * 2, BF16, (GS, NX))
    NTA = NT - (MIX_OUT_START[0] - A_BACK)
    apad = carve(o, 2 * NTA * 2, BF16, (2, NTA)); o += 2 * NTA * 2
    diag = carve(o, CONVW * 128 * 2, BF16, (CONVW, 128)); o += CONVW * 128 * 2
    T1 = carve(o, 4096, F32, (8, 128)); o += 4096
    wct = carve(o, 2048, BF16, (8, 128)); o += 2048
    vT = carve(o, 4096, BF16, (2, 1024)); o += 4096
    tmpf = carve(o, 4 * 2048, F32, (4, 512)); o += 4 * 2048
    tmpb = carve(o, 2 * 1024, BF16, (2, 512)); o += 2 * 1024
    assert o <= SCR_BYTES, o
    NH = NT - MIX_H_START[1]
    po = 0
    p_rstd = carve(po, NH * 4, F32, None); po += NH * 4
    psets = []
    for _si in range(2):
        st_ = []
        for _k in range(3):
            st_.append(carve(po, NH * 4, F32, None)); po += NH * 4
        psets.append(st_)
    assert po <= o - 4 * 2048 - 2 * 1024, (po, o)
    rstd_all = carve(9344, NT * 4, F32, None)
    xh0 = carve(0, 16 * XO * 4, F32, (16, XO))
    wstmp = carve(CAT_B, 4096, F32, (8, 128))

    P = [nc.alloc_psum_tensor(f"ps{i}", [128, 512], F32) for i in range(8)]
    PB = [Buf(f"ps{i}") for i in range(8)]

    PE = Eng(nc, nc.tensor, "pe")
    ACT = Eng(nc, nc.scalar, "act")
    DVE = Eng(nc, nc.vector, "dve")
    POOL = Eng(nc, nc.gpsimd, "pool")
    SP = Eng(nc, nc.sync, "sp")

    xb = [RBuf(f"x{f}") for f in range(16)]
    hbb = [Buf(f"hb{f}") for f in range(16)]
    hbt = [Buf(f"hbt{i}") for i in range(4)]
    slotb = [Buf(f"slot{s}", nc, dma=True) for s in range(NSLOT)]
    xload = Buf("xload", nc, dma=True)
    consts = Buf("consts", nc, dma=True)
    catb = [RBuf(f"cat{c}") for c in range(8)]
    apadb = [Buf("apad0"), Buf("apad1")]
    diagb = Buf("diag")
    diagb2 = Buf("diag2")
    t1b = Buf("T1", nc, dma=True)
    wctb = Buf("wct")
    wstb = Buf("wstmp", nc, dma=True)
    vTb = [Buf("vT0"), Buf("vT1")]
    tmpfb = [Buf(f"tmpf{i}") for i in range(4)]
    tmpbb = [Buf(f"tmpb{i}") for i in range(2)]
    modb = [Buf("mod0"), Buf("mod1")]
    vecb = [Buf("vec0"), Buf("vec1")]
    stb = Buf("st")
    condb = Buf("cond")
    miscb = Buf("misc")
    poolb = Buf("poolscr")
    rstdallb = Buf("rstdall")
    psetb = [Buf("pset0"), Buf("pset1")]
    outb = Buf("outst", nc, dma=True)

    class Ring:
        def __init__(self):
            self.issued = 0
            self.next = 0
            self.released = [False] * NU

        def pump(self):
            while self.issued < NU and (self.issued < NSLOT or self.released[self.issued - NSLOT]):
                u = self.issued
                s = u % NSLOT
                dma_load(POOL, slotb[s], ring_t[:, s, :], d_w[u])
                self.issued += 1

        def get(self, spec):
            u = self.next
            assert plan[u] == spec, (u, plan[u], spec)
            self.pump()
            assert self.issued > u, ("ring stall", u, spec)
            self.next += 1
            return u

        def release(self, u):
            self.released[u] = True
            self.pump()

    W = Ring()

    def slot_ap(u):
        return ring_t[:, u % NSLOT, :]

    def slot_buf(u):
        return slotb[u % NSLOT]

    def xs(fc, t0, T):
        return x_t[:, fc, t0 - XO:t0 - XO + T]

    def hs_(fc, t0, T):
        return hb_t[:, fc, t0:t0 + T]

    def sm(off, n=1):
        return sm_t[:, off:off + n]

    def mm_group(ps_i, T, pairs, reads, Mrows=128, col0=0, rec_only=()):
        rl = list(_items(reads))
        for b, lo, hi in rl:
            for t in b.deps(False, lo, hi):
                PE.wait(t)
        pb = PB[ps_i]
        for t in pb.deps(True):
            PE.wait(t)
        n = len(pairs)
        ins = None
        for idx, (l_ap, r_ap) in enumerate(pairs):
            ins = nc.tensor.matmul(P[ps_i][:Mrows, col0:col0 + T], l_ap, r_ap,
                                   start=(idx == 0), stop=(idx == n - 1))
        tok = PE.signal(ins)
        for b, lo, hi in rl:
            b.record(tok, False, "pe", lo, hi)
        for b, lo, hi in _items(rec_only):
            b.record(tok, False, "pe", lo, hi)
        pb.record(tok, True, "pe")
        return tok

    for fc in range(16):
        dma_load(SP, xload, x_t[:, fc, :], d_x[fc][:, XO:])
    for fc in range(16):
        dma_load(SP, consts, xh0[:, fc, :], d_x[fc][:, 0:XO])
    dma_load(SP, consts, sm_t[:], d_sm)
    dma_load(SP, consts, hmask_t[:], d_hmask)
    dma_load(SP, consts, invc_t[:], d_invc)
    dma_load(SP, consts, ident_t[:], d_ident)
    dma_load(SP, consts, cmask_t[:], d_cmask)
    for f in range(16):
        xb[f].set_all(xload.w, 0, NT)
    op(DVE, lambda: nc.vector.memset(ones_t[:], 1.0), writes=[miscb])
    op(DVE, lambda: nc.vector.memset(eps_t[:], EPS), writes=[miscb])
    op(DVE, lambda: nc.vector.tensor_copy(out=identb_t[:], in_=ident_t[:]), reads=[consts], writes=[miscb])
    op(ACT, lambda: nc.scalar.activation(out=cond_t[:], in_=sm(S_CVEC, 16), func=AF.Silu),
       reads=[consts], writes=[condb])
    W.pump()

    def ada_chunks(l, c_lo, c_hi, bank=7, evac=True):
        par = l % 2
        if c_hi <= c_lo:
            return
        for c in range(c_lo, c_hi):
            u = W.get(("ada", l, c))
            ua = slot_ap(u).rearrange("p (k m) -> p k m", k=16)
            pairs = [(ua[:, k, :], cond_t[:, k:k + 1]) for k in range(16)]
            for b in (slot_buf(u), condb):
                PE.wait(b.w)
            pb = PB[bank]
            if c == c_lo:
                PE.wait(pb.w)
                for t in pb.r.values():
                    PE.wait(t)
            ins = None
            for k in range(16):
                ins = nc.tensor.matmul(P[bank][:, c:c + 1], pairs[k][0], pairs[k][1],
                                       start=(k == 0), stop=(k == 15))
            tok = PE.signal(ins)
            slot_buf(u).r["pe"] = tok
            condb.r["pe"] = tok
            pb.w = tok
            pb.r = {}
            W.release(u)
        if evac:
            ada_evac(l, c_lo, c_hi, bank)

    def ada_evac(l, c_lo, c_hi, bank):
        par = l % 2
        op(DVE, lambda: nc.vector.tensor_tensor(out=mod_t[:, par, c_lo:c_hi], in0=P[bank][:, c_lo:c_hi],
                                                in1=sm(S_ADAB + l * 96 + c_lo, c_hi - c_lo), op=ALU.add),
           reads=[PB[bank], consts], writes=[modb[par]])

    def ada_vectors(l, part=None):
        par = l % 2
        i = l // 2
        if part == 'b':
            return ada_vectors_b(l)
        op(DVE, lambda: nc.vector.scalar_tensor_tensor(out=vec_t[:, par, 0, :], in0=mod_t[:, par, 16:32], scalar=1.0,
                                                       in1=sm(S_GMIX + l * 16, 16), op0=ALU.add, op1=ALU.mult),
           reads=[modb[par], consts], writes=[vecb[par]])
        if part == 'a':
            return
        ada_vectors_b(l)

    def ada_vectors_b(l):
        par = l % 2
        i = l // 2
        if l % 2 == 1:
            op(DVE, lambda: nc.vector.tensor_tensor(out=vec_t[:, par, 1, :], in0=mod_t[:, par, 32:48],
                                                    in1=sm(S_PSC + i * 16, 16), op=ALU.mult),
               reads=[modb[par], consts], writes=[vecb[par]])
        else:
            op(DVE, lambda: nc.vector.tensor_copy(out=vec_t[:, par, 1, :], in_=mod_t[:, par, 32:48]),
               reads=[modb[par]], writes=[vecb[par]])
        op(DVE, lambda: nc.vector.scalar_tensor_tensor(out=vec_t[:, par, 2, :], in0=mod_t[:, par, 64:80], scalar=1.0,
                                                       in1=sm(S_GFFN + l * 16, 16), op0=ALU.add, op1=ALU.mult),
           reads=[modb[par], consts], writes=[vecb[par]])

    def A1(l, f): return vec_t[:, l % 2, 0, f:f + 1]
    def B1(l, f): return mod_t[:, l % 2, f:f + 1]
    def G1(l, f): return vec_t[:, l % 2, 1, f:f + 1]
    def A2(l, f): return vec_t[:, l % 2, 2, f:f + 1]
    def B2(l, f): return mod_t[:, l % 2, 48 + f:48 + f + 1]
    def G2(l, f): return mod_t[:, l % 2, 80 + f:80 + f + 1]

    def rms_rstd(src, src_bufs, t0, T, dst_ap, dst_buf):
        for fc in range(16):
            sq = hs_(fc, t0, T)
            sqb = hbb[fc]
            s_ap = src(fc, t0, T)
            if fc % 2 == 0:
                op(ACT, lambda: nc.scalar.activation(out=sq, in_=s_ap, func=AF.Square),
                   reads=[(src_bufs[fc], t0, t0 + T)], writes=[sqb])
            else:
                op(DVE, lambda: nc.vector.tensor_tensor(out=sq, in0=s_ap, in1=s_ap, op=ALU.mult),
                   reads=[(src_bufs[fc], t0, t0 + T)], writes=[sqb])
        for fc in range(16):
            sq = hs_(fc, t0, T)
            sqb = hbb[fc]
            PE.wait(sqb.w)
            PE.wait(miscb.w)
            if fc == 0:
                PE.wait(PB[6].w)
                for t in PB[6].r.values():
                    PE.wait(t)
            ins = nc.tensor.matmul(P[6][:, :T], ones_t[:], sq, start=(fc == 0), stop=(fc == 15))
            if fc == 15:
                tok = PE.signal(ins)
                for b_ in hbb:
                    b_.r["pe"] = tok
                PB[6].w = tok
                PB[6].r = {}
        op(ACT, lambda: nc.scalar.activation(out=dst_ap, in_=P[6][:, :T], func=AF.Sqrt,
                                             bias=eps_t[:, 0:1], scale=1.0 / D),
           reads=[PB[6], miscb], writes=[dst_buf])
        op(DVE, lambda: nc.vector.reciprocal(out=dst_ap, in_=dst_ap), reads=[dst_buf], writes=[dst_buf])

    def norm_phase(tile_list, src, src_bufs, Afn, Bfn, extra_reads, pre=None, after_stats=None):
        for ti_, (t0, T) in enumerate(tile_list):
            if pre is None:
                rstd = tmpf[:, 0, :T]
                rstd_buf = tmpfb[0]
                rms_rstd(src, src_bufs, t0, T, rstd, rstd_buf)
                if after_stats is not None:
                    after_stats(ti_)
            else:
                rstd = pre[0][:, t0:t0 + T]
                rstd_buf = pre[1]
            for fc in range(16):
                tb = 1 + fc % 2
                tt = tmpf[:, tb, :T]
                s_ap = src(fc, t0, T)
                op(DVE, lambda: nc.vector.tensor_tensor(out=tt, in0=s_ap, in1=rstd, op=ALU.mult),
                   reads=[(src_bufs[fc], t0, t0 + T), rstd_buf], writes=[tmpfb[tb]])
                op(ACT, lambda: nc.scalar.activation(out=hs_(fc, t0, T), in_=tt, func=AF.Identity,
                                                     bias=Bfn(fc), scale=Afn(fc)),
                   reads=[tmpfb[tb]] + extra_reads, writes=[hbb[fc]])
            hbt[ti_].w = Tok(ACT, ACT.sem, ACT.count, ACT.name)
            hbt[ti_].r = {}

    def x_update(f, t0, T, ps_i, gate_ap, extra_reads):
        op(DVE, lambda: nc.vector.scalar_tensor_tensor(out=xs(f, t0, T), in0=P[ps_i][:, :T], scalar=gate_ap,
                                                       in1=xs(f, t0, T), op0=ALU.mult, op1=ALU.add),
           reads=[PB[ps_i]] + extra_reads, writes=[(xb[f], t0, t0 + T)])

    def ffn_phase(l):
        par = l % 2
        s = MIX_OUT_START[l]
        tl = tiles(s, NT)
        if l + 1 < DEPTH:
            assert len(tl) == 3
            ab0 = ada_ffn_base(l)
            norm_phase(tl, xs, xb, lambda f: A2(l, f), lambda f: B2(l, f), [vecb[par], modb[par]],
                       after_stats=lambda ti: ada_chunks(l + 1, ab0 + 5 * ti, ab0 + 5 * ti + 5, bank=7, evac=False))
            ada_evac(l + 1, ab0, ab0 + NORM_FILL, 7)
        else:
            norm_phase(tl, xs, xb, lambda f: A2(l, f), lambda f: B2(l, f), [vecb[par], modb[par]])
        gbufs = catb[:GS]
        ada_c = 0
        pp = 0
        def a_step(jj, u1, u3, t0, T, hreads, rec):
            nonlocal pp
            a1 = slot_ap(u1).rearrange("p (k m) -> p k m", k=16)
            a3 = slot_ap(u3).rearrange("p (k m) -> p k m", k=16)
            pa, pb_ = (0, 1) if pp % 2 == 0 else (2, 3)
            pp += 1
            mm_group(pa, T, [(a1[:, k, :], hs_(k, t0, T)) for k in range(16)], [slot_buf(u1)] + hreads, rec_only=rec)
            mm_group(pb_, T, [(a3[:, k, :], hs_(k, t0, T)) for k in range(16)], [slot_buf(u3)] + hreads, rec_only=rec)
            tb = 2 + (pp % 2)
            sg = tmpf[:, tb, :T]
            op(ACT, lambda: nc.scalar.activation(out=sg, in_=P[pa][:, :T], func=AF.Silu),
               reads=[PB[pa]], writes=[tmpfb[tb]])
            op(DVE, lambda: nc.vector.tensor_tensor(out=gbuf[:, jj, t0 - XO:t0 - XO + T], in0=sg,
                                                    in1=P[pb_][:, :T], op=ALU.mult),
               reads=[tmpfb[tb], PB[pb_]], writes=[(gbufs[jj], t0, t0 + T)])

        for q in range(NG):
            if q == 0:
                us0 = []
                for jj in range(GS):
                    us0.append((W.get(("w1", l, jj)), W.get(("w3", l, jj))))
                for ti, (t0, T) in enumerate(tl):
                    for jj in range(GS):
                        u1, u3 = us0[jj]
                        a_step(jj, u1, u3, t0, T, [hbt[ti]], hbb)
                        if ti == len(tl) - 1:
                            W.release(u1)
                            W.release(u3)
            else:
                for jj in range(GS):
                    j = q * GS + jj
                    u1 = W.get(("w1", l, j))
                    u3 = W.get(("w3", l, j))
                    for (t0, T) in tl:
                        a_step(jj, u1, u3, t0, T, hbb, ())
                    W.release(u1)
                    W.release(u3)
                    if l + 1 < DEPTH:
                        alo, ahi = ada_split(j, ada_ffn_base(l) + NORM_FILL)
                        ada_chunks(l + 1, alo, ahi, bank=6 + j % 2)
            us = [W.get(("w2", l, q * GS + jj)) for jj in range(GS)]
            pc = 0
            for f in range(16):
                for (t0, T) in tl:
                    pi = (4, 5, 0, 1, 2, 3)[pc % 6]
                    pc += 1
                    mm_group(pi, T, [(slot_ap(us[jj])[:, f * 128:(f + 1) * 128], gbuf[:, jj, t0 - XO:t0 - XO + T])
                                     for jj in range(GS)],
                             [slot_buf(u) for u in us] + [(gb_, t0, t0 + T) for gb_ in gbufs])
                    x_update(f, t0, T, pi, G2(l, f), [modb[par]])
            for u in us:
                W.release(u)
        if l + 1 < DEPTH:
            ada_vectors(l + 1)

    def mixer_ab(l):
        stage = stage_in if l == n_layers - 1 else 99
        i = l // 2
        par = l % 2
        hs0 = MIX_H_START[l]
        so = MIX_OUT_START[l]
        a0 = so - A_BACK
        nta = NT - a0
        cw = S_CONVW + i * 8 * CONVW

        if l == 0:
            norm_phase([(0, XO)], src0, [consts] * 16, lambda f: A1(l, f), lambda f: B1(l, f), [vecb[par], modb[par]],
                       pre=(rstd_all, rstdallb))
            norm_phase(tiles(XO, NT), xs, xb, lambda f: A1(l, f), lambda f: B1(l, f), [vecb[par], modb[par]],
                       pre=(rstd_all, rstdallb))
        else:
            ntl = tiles(hs0, NT)
            if l == 2 and DEPTH > 3:
                assert len(ntl) == 3
                norm_phase(ntl, xs, xb, lambda f: A1(l, f), lambda f: B1(l, f), [vecb[par], modb[par]],
                           after_stats=lambda ti: ada_chunks(3, 5 * ti, 5 * ti + 5, bank=7, evac=False))
                ada_evac(3, 0, NORM_FILL, 7)
            else:
                norm_phase(ntl, xs, xb, lambda f: A1(l, f), lambda f: B1(l, f), [vecb[par], modb[par]])

        if stage <= 1:
            return
        for e_ in (PE, ACT, DVE):
            if e_.count > 0:
                SP.wait(Tok(e_, e_.sem, e_.count, e_.name))
        dma_load(SP, wstb, wstmp.rearrange("p a b -> p (a b)"), d_wsT[i])
        dma_load(SP, t1b, T1.rearrange("p a b -> p (a b)"), d_bbias[i])
        for h in range(8):
            op(DVE, lambda: nc.vector.tensor_tensor(out=wct[:, h, :], in0=wstmp[:, h, :], in1=cmask_t[:], op=ALU.mult),
               reads=[wstb, consts], writes=[wctb])
        wflat = wct.rearrange("p a b -> p (a b)")
        mm_group(0, 512, [(ones_t[:], wflat[:, 0:512])], [wctb, miscb])
        mm_group(1, 512, [(ones_t[:], wflat[:, 512:1024])], [wctb, miscb])
        for h in range(8):
            pi = h // 4
            op(DVE, lambda: nc.vector.scalar_tensor_tensor(out=T1[:, h, :], in0=P[pi][:, (h % 4) * 128:(h % 4 + 1) * 128],
                                                           scalar=sm(S_VB + i * 8 + h), in1=T1[:, h, :],
                                                           op0=ALU.mult, op1=ALU.add),
               reads=[PB[pi], t1b, consts], writes=[t1b])

        tla = tiles(a0, NT)
        tlo = tiles(so, NT)
        pp = 0

        def conv(c):
            nonlocal pp
            for (t0, T) in tlo:
                pi = 4 + pp % 2
                pp += 1
                base = (t0 - so) + 2
                mm_group(pi, T, [(diag[:, j, :], apad[:, c % 2, base + j:base + j + T]) for j in range(CONVW)],
                         [diagb, diagb2, apadb[c % 2]])
                op(ACT, lambda: nc.scalar.activation(out=cat[:, c, t0 - XO:t0 - XO + T], in_=P[pi][:, :T],
                                                     func=AF.Identity, bias=sm(S_CONVB + i * 8 + c), scale=1.0),
                   reads=[PB[pi], consts], writes=[(catb[c], t0, t0 + T)])

        def build_diag(c):
            for j in range(CONVW):
                if j % 2 == 0:
                    op(ACT, lambda: nc.scalar.activation(out=diag[:, j, :], in_=ident_t[:], func=AF.Identity,
                                                         scale=sm(cw + c * CONVW + j)),
                       reads=[consts], writes=[diagb])
                else:
                    op(DVE, lambda: nc.vector.tensor_scalar(out=diag[:, j, :], in0=ident_t[:],
                                                            scalar1=sm(cw + c * CONVW + j), scalar2=None, op0=ALU.mult),
                       reads=[consts], writes=[diagb2])

        pq = 0
        for c in range(8):
            uv = W.get(("win", i, c))
            ug = W.get(("win", i, 8 + c))
            av = slot_ap(uv).rearrange("p (k m) -> p k m", k=16)
            ag = slot_ap(ug).rearrange("p (k m) -> p k m", k=16)
            for (t0, T) in tla:
                pa, pb_ = (0, 1) if pq % 2 == 0 else (2, 3)
                pq += 1
                mm_group(pa, T, [(av[:, k, :], hs_(k, t0, T)) for k in range(16)], [slot_buf(uv)] + hbb)
                mm_group(pb_, T, [(ag[:, k, :], hs_(k, t0, T)) for k in range(16)], [slot_buf(ug)] + hbb)
                tb = 2 + (pq % 2)
                sg = tmpf[:, tb, :T]
                op(ACT, lambda: nc.scalar.activation(out=sg, in_=P[pb_][:, :T], func=AF.Sigmoid),
                   reads=[PB[pb_]], writes=[tmpfb[tb]])
                op(DVE, lambda: nc.vector.tensor_tensor(out=apad[:, c % 2, t0 - a0:t0 - a0 + T], in0=sg,
                                                        in1=P[pa][:, :T], op=ALU.mult),
                   reads=[tmpfb[tb], PB[pa]], writes=[apadb[c % 2]])
            nh = HALO - a0
            op(DVE, lambda: nc.vector.tensor_tensor(out=apad[:, c % 2, 0:nh], in0=apad[:, c % 2, 0:nh],
                                                    in1=hmask_t[:, 0:nh], op=ALU.mult),
               reads=[consts], writes=[apadb[c % 2]])
            W.release(uv)
            W.release(ug)
            if l == 0:
                ada_chunks(0, 32 + 8 * c, 40 + 8 * c, bank=6 + c % 2)
            if c >= 1:
                conv(c - 1)
            build_diag(c)
        conv(7)
        if l == 0:
            ada_vectors(0, 'b')

        if stage <= 2:
            return
        for (t0, T) in tlo:
            for c in range(8):
                sq = tmpb[:, c % 2, :T]
                sqb = tmpbb[c % 2]
                c_ap = cat[:, c, t0 - XO:t0 - XO + T]
                if c % 2 == 0:
                    op(ACT, lambda: nc.scalar.activation(out=sq, in_=c_ap, func=AF.Square),
                       reads=[(catb[c], t0, t0 + T)], writes=[sqb])
                else:
                    op(DVE, lambda: nc.vector.tensor_tensor(out=sq, in0=c_ap, in1=c_ap, op=ALU.mult),
                       reads=[(catb[c], t0, t0 + T)], writes=[sqb])
                for b in (sqb, miscb):
                    PE.wait(b.w)
                for t in catb[c].deps(False, t0, t0 + T):
                    PE.wait(t)
                if c == 0:
                    for pi in (6, 7):
                        PE.wait(PB[pi].w)
                        for t in PB[pi].r.values():
                            PE.wait(t)
                nc.tensor.matmul(P[6][:, :T], ones_t[:], c_ap, start=(c == 0), stop=(c == 7))
                ins = nc.tensor.matmul(P[7][:, :T], ones_t[:], sq, start=(c == 0), stop=(c == 7))
                tok = PE.signal(ins)
                sqb.r["pe"] = tok
                catb[c].record(tok, False, "pe", t0, t0 + T)
                for pi in (6, 7):
                    PB[pi].w = tok
                    PB[pi].r = {}
            if l == 2 and DEPTH > 3:
                ti_ln = tlo.index((t0, T))
                ada_chunks(3, NORM_FILL + 5 * ti_ln, NORM_FILL + 5 * ti_ln + 5, bank=0, evac=False)
            mu = tmpf[:, 0, :T]
            tB = tmpf[:, 1, :T]
            op(DVE, lambda: nc.vector.tensor_scalar(out=mu, in0=P[6][:, :T], scalar1=1.0 / 1024, scalar2=None, op0=ALU.mult),
               reads=[PB[6]], writes=[tmpfb[0]])
            op(DVE, lambda: nc.vector.tensor_tensor(out=tB, in0=mu, in1=mu, op=ALU.mult),
               reads=[tmpfb[0]], writes=[tmpfb[1]])
            op(DVE, lambda: nc.vector.scalar_tensor_tensor(out=tB, in0=P[7][:, :T], scalar=1.0 / 1024, in1=tB,
                                                           op0=ALU.mult, op1=ALU.subtract),
               reads=[PB[7]], writes=[tmpfb[1]])
            op(ACT, lambda: nc.scalar.activation(out=tB, in_=tB, func=AF.Sqrt, bias=eps_t[:, 0:1], scale=1.0),
               reads=[tmpfb[1], miscb], writes=[tmpfb[1]])
            op(DVE, lambda: nc.vector.reciprocal(out=tB, in_=tB), reads=[tmpfb[1]], writes=[tmpfb[1]])
            op(DVE, lambda: nc.vector.tensor_tensor(out=mu, in0=mu, in1=tB, op=ALU.mult),
               reads=[tmpfb[1]], writes=[tmpfb[0]])
            for c in range(8):
                tb = 2 + c % 2
                u_ap = tmpf[:, tb, :T]
                c_ap = cat[:, c, t0 - XO:t0 - XO + T]
                op(DVE, lambda: nc.vector.tensor_tensor(out=u_ap, in0=c_ap, in1=tB, op=ALU.mult),
                   reads=[(catb[c], t0, t0 + T), tmpfb[1]], writes=[tmpfb[tb]])
                op(DVE, lambda: nc.vector.tensor_tensor(out=u_ap, in0=u_ap, in1=mu, op=ALU.subtract),
                   reads=[tmpfb[0]], writes=[tmpfb[tb]])
                op(ACT, lambda: nc.scalar.activation(out=c_ap, in_=u_ap, func=AF.Silu,
                                                     bias=sm(S_AB + i * 8 + c), scale=sm(S_AG + i * 8 + c)),
                   reads=[tmpfb[tb], consts], writes=[(catb[c], t0, t0 + T)])

        if stage <= 3:
            return
        if l == 2 and DEPTH > 3:
            assert len(tlo) == 3
            ada_evac(3, NORM_FILL, 2 * NORM_FILL, 0)
        def wout_part(part):
            pc = 0
            for fp in range(8):
                u = W.get(("wout", i, part, fp))
                ua = slot_ap(u).rearrange("p (f c m) -> p f c m", f=2, c=8)
                for ff in range(2):
                    f = 2 * fp + ff
                    for (t0, T) in tlo:
                        pi = (4, 5, 0, 1, 2, 3)[pc % 6]
                        pc += 1
                        mm_group(pi, T, [(ua[:, ff, c, :], cat[:, c, t0 - XO:t0 - XO + T]) for c in range(8)],
                                 [slot_buf(u)] + [(cb_, t0, t0 + T) for cb_ in catb])
                        x_update(f, t0, T, pi, G1(l, f), [vecb[par]])
                W.release(u)

        wout_part(0)
        if stage <= 4:
            return

        pq = 0
        for c in range(8):
            u = W.get(("win", i, 16 + c))
            ua = slot_ap(u).rearrange("p (k m) -> p k m", k=16)
            for (t0, T) in tlo:
                pa = pq % 2
                pq += 1
                mm_group(pa, T, [(ua[:, k, :], hs_(k, t0, T)) for k in range(16)], [slot_buf(u)] + hbb)
                op(ACT, lambda: nc.scalar.activation(out=cat[:, c, t0 - XO:t0 - XO + T], in_=P[pa][:, :T], func=AF.Identity),
                   reads=[PB[pa]], writes=[(catb[c], t0, t0 + T)])
            W.release(u)

        if stage <= 5:
            return
        uvs = [[W.get(("winv", i, half, kg)) for kg in range(4)] for half in range(2)]
        n0 = hs0 // 128
        NCH = NT // 128

        def bank_of(n, half):
            return (0, 1)[half] if (n - n0) % 2 == 0 else (4, 5)[half]

        def bv_proj(n):
            tk = n * 128
            for half in range(2):
                pairs = []
                for k in range(16):
                    ua = slot_ap(uvs[half][k // 4]).rearrange("p (k m) -> p k m", k=4)
                    pairs.append((hb_t[:, k, tk:tk + 128], ua[:, k % 4, :]))
                mm_group(bank_of(n, half), 512, pairs, [slot_buf(u) for u in uvs[half]] + hbb)

        bv_proj(n0)
        for n in range(n0, NCH):
            tk = n * 128
            if n + 1 < NCH:
                bv_proj(n + 1)
            op(DVE, lambda: nc.vector.memset(st_t[:, 0:4], 0.0), writes=[stb])
            for half in range(2):
                bk = bank_of(n, half)
                junk = tmpf[:, 2 + half, :]
                op(ACT, lambda: nc.scalar.activation(out=junk, in_=P[bk][:, :], func=AF.Identity,
                                                     accum_out=st_t[:, half:half + 1]),
                   writes=[PB[bk], tmpfb[2 + half], stb])
                op(ACT, lambda: nc.scalar.activation(out=junk, in_=P[bk][:, :], func=AF.Square,
                                                     accum_out=st_t[:, 2 + half:3 + half]),
                   writes=[PB[bk], tmpfb[2 + half], stb])
            op(DVE, lambda: nc.vector.tensor_tensor(out=st_t[:, 4:5], in0=st_t[:, 0:1], in1=st_t[:, 1:2], op=ALU.add), writes=[stb])
            op(DVE, lambda: nc.vector.tensor_tensor(out=st_t[:, 5:6], in0=st_t[:, 2:3], in1=st_t[:, 3:4], op=ALU.add), writes=[stb])
            op(DVE, lambda: nc.vector.tensor_scalar(out=st_t[:, 4:6], in0=st_t[:, 4:6], scalar1=1.0 / 1024, scalar2=None, op0=ALU.mult), writes=[stb])
            op(DVE, lambda: nc.vector.tensor_tensor(out=st_t[:, 6:7], in0=st_t[:, 4:5], in1=st_t[:, 4:5], op=ALU.mult), writes=[stb])
            op(DVE, lambda: nc.vector.tensor_tensor(out=st_t[:, 6:7], in0=st_t[:, 5:6], in1=st_t[:, 6:7], op=ALU.subtract), writes=[stb])
            op(ACT, lambda: nc.scalar.activation(out=st_t[:, 7:8], in_=st_t[:, 6:7], func=AF.Sqrt, bias=eps_t[:, 0:1], scale=1.0),
               reads=[stb, miscb], writes=[stb])
            op(DVE, lambda: nc.vector.reciprocal(out=st_t[:, 7:8], in_=st_t[:, 7:8]), reads=[stb], writes=[stb])
            op(DVE, lambda: nc.vector.scalar_tensor_tensor(out=st_t[:, 8:9], in0=st_t[:, 4:5], scalar=-1.0, in1=st_t[:, 7:8],
                                                           op0=ALU.mult, op1=ALU.mult), writes=[stb])
            vb = vTb[n % 2]
            for half in range(2):
                bk = bank_of(n, half)
                op(ACT, lambda: nc.scalar.activation(out=vT[:, n % 2, half * 512:(half + 1) * 512], in_=P[bk][:, :],
                                                     func=AF.Identity, bias=st_t[:, 8:9], scale=st_t[:, 7:8]),
                   reads=[stb], writes=[vb, PB[bk]])
            for hh in range(2):
                for b in (vb, wctb):
                    PE.wait(b.w)
                pb = PB[2 + hh]
                PE.wait(pb.w)
                for t in pb.r.values():
                    PE.wait(t)
                ins = None
                for h4 in range(4):
                    h = hh * 4 + h4
                    ins = nc.tensor.matmul(P[2 + hh][:, h4 * 128:(h4 + 1) * 128], vT[:, n % 2, h * 128:(h + 1) * 128],
                                           wct[:, h, :], start=True, stop=True)
                tok = PE.signal(ins)
                vb.r["pe"] = tok
                wctb.r["pe"] = tok
                pb.w = tok
                pb.r = {}
            lo = max(tk, so)
            if lo < tk + 128:
                w_ = tk + 128 - lo
                tm8 = tmpf[:, 0:2, :].rearrange("p a (h t) -> p (a h) t", h=4)
                for h in range(8):
                    op(DVE, lambda: nc.vector.scalar_tensor_tensor(
                        out=tm8[:, h, lo - tk:128], in0=P[2 + h // 4][:, (h % 4) * 128 + lo - tk:(h % 4) * 128 + 128],
                        scalar=sm(S_VG + i * 8 + h), in1=T1[:, h, lo - tk:128], op0=ALU.mult, op1=ALU.add),
                       reads=[PB[2 + h // 4], t1b, consts], writes=[tmpfb[h // 4]])
                op(DVE, lambda: nc.vector.tensor_tensor(out=cat[:, :, lo - XO:tk + 128 - XO], in0=tm8[:, :, lo - tk:128],
                                                        in1=cat[:, :, lo - XO:tk + 128 - XO], op=ALU.mult),
                   reads=[tmpfb[0], tmpfb[1]], writes=[(catb[h_], lo, tk + 128) for h_ in range(8)])
        for half in range(2):
            for u in uvs[half]:
                W.release(u)

        if stage <= 6:
            if dbg:
                for c in range(8):
                    op(DVE, lambda: nc.vector.tensor_copy(out=x_t[:, c, :], in_=cat[:, c, :]), reads=[(catb[c], 0, NT)], writes=[(xb[c], 0, NT)])
            return
        wout_part(1)

    def mixer_c(l):
        i = l // 2
        par = l % 2
        hs0 = MIX_H_START[l]
        so = MIX_OUT_START[l]
        nh = NT - hs0
        tl = tiles(hs0, NT)
        for ti_, (t0, T) in enumerate(tl):
            rms_rstd(xs, xb, t0, T, p_rstd[:, t0 - hs0:t0 - hs0 + T], poolb)
            if l + 1 < DEPTH:
                assert len(tl) == 3
                ada_chunks(l + 1, 5 * ti_, 5 * ti_ + 5, bank=7, evac=False)
        if l + 1 < DEPTH:
            ada_evac(l + 1, 0, NORM_FILL, 7)
        nhal = HALO - hs0
        tlo_ = tiles(so, NT)
        pk = 0

        def s1(fc):
            si = fc % 2
            sa_, h32, h16f = psets[si]
            h16 = h16f.bitcast(BF16)
            sbuf_ = psetb[si]
            op(DVE, lambda: nc.vector.tensor_tensor(out=sa_[:, :nh], in0=xs(fc, hs0, nh), in1=p_rstd[:, :nh], op=ALU.mult),
               reads=[(xb[fc], hs0, NT), poolb], writes=[sbuf_])
            op(ACT, lambda: nc.scalar.activation(out=h32[:, :nh], in_=sa_[:, :nh], func=AF.Identity,
                                                 bias=B1(l, fc), scale=A1(l, fc)),
               reads=[vecb[par], modb[par]], writes=[sbuf_])
            op(POOL, lambda: nc.gpsimd.tensor_tensor(out=h32[:, :nhal], in0=h32[:, :nhal], in1=hmask_t[:, :nhal], op=ALU.mult),
               reads=[consts], writes=[sbuf_])
            op(ACT, lambda: nc.scalar.activation(out=h16[:, :nh], in_=h32[:, :nh], func=AF.Identity),
               writes=[sbuf_])

        def s2(fc):
            nonlocal pk
            g = fc // 4
            w = POOLW[g]
            si = fc % 2
            sa_, h32, h16f = psets[si]
            h16 = h16f.bitcast(BF16)
            sbuf_ = psetb[si]
            for (t0, T) in tlo_:
                pi = 4 + pk % 2
                pk += 1
                b0 = t0 - hs0
                mm_group(pi, T, [(identb_t[:], h16[:, b0 - k:b0 - k + T]) for k in range(w)], [sbuf_, miscb])
                op(DVE, lambda: nc.vector.scalar_tensor_tensor(out=hb_t[:, fc, t0:t0 + T], in0=P[pi][:, :T], scalar=1.0 / w,
                                                               in1=h32[:, b0:b0 + T], op0=ALU.mult, op1=ALU.subtract),
                   reads=[PB[pi], sbuf_], writes=[hbb[fc]])
                if t0 <= HALO < t0 + T:
                    off = HALO - t0
                    assert off + 16 <= T
                    f0 = HALO - hs0
                    t16 = tmpf[:, 0, :16]
                    op(DVE, lambda: nc.vector.tensor_tensor(out=t16, in0=P[pi][:, off:off + 16],
                                                            in1=invc_t[:, g * 16:(g + 1) * 16], op=ALU.mult),
                       reads=[PB[pi], consts], writes=[tmpfb[0]])
                    op(DVE, lambda: nc.vector.tensor_tensor(out=hb_t[:, fc, HALO:HALO + 16], in0=t16,
                                                            in1=h32[:, f0:f0 + 16], op=ALU.subtract),
                       reads=[tmpfb[0], sbuf_], writes=[hbb[fc]])

        s1(0)
        for fc in range(16):
            if fc + 1 < 16:
                s1(fc + 1)
            s2(fc)
        tlo = tiles(so, NT)
        pc = 0
        for g in range(4):
            u = W.get(("pool", i, g))
            ua = slot_ap(u).rearrange("p (k m) -> p k m", k=4)
            for oc in range(4):
                f = g * 4 + oc
                for (t0, T) in tlo:
                    pi = (4, 5, 0, 1, 2, 3)[pc % 6]
                    pc += 1
                    mm_group(pi, T, [(ua[:, ic, oc * 128:(oc + 1) * 128], hs_(g * 4 + ic, t0, T)) for ic in range(4)],
                             [slot_buf(u)] + hbb[g * 4:g * 4 + 4])
                    x_update(f, t0, T, pi, G1(l, f), [vecb[par]])
            W.release(u)

    def src0(fc, t0, T):
        if t0 < XO:
            assert t0 + T <= XO
            return xh0[:, fc, t0:t0 + T]
        return xs(fc, t0, T)

    ada_chunks(0, 0, 32)
    if n_layers >= 1:
        rms_rstd(src0, [consts] * 16, 0, XO, rstd_all[:, 0:XO], rstdallb)
        for (t0_, T_) in tiles(XO, NT):
            rms_rstd(xs, xb, t0_, T_, rstd_all[:, t0_:t0_ + T_], rstdallb)
    ada_vectors(0, 'a')
    for l in range(n_layers):
        if l % 2 == 0:
            mixer_ab(l)
        else:
            mixer_c(l)
        if stage_in <= 7 and l == n_layers - 1:
            break
        ffn_phase(l)

    if dbg:
        dbgb = Buf("dbgst", nc, dma=True)
        for fc in range(16):
            dma_load(SP, dbgb, d_dbg[fc], x_t[:, fc, :], reads=[(xb[fc], 0, NT)])
        SP.wait(dbgb.w)

    for (t0, T) in tiles(HALO, NT):
        rstd = tmpf[:, 0, :T]
        rms_rstd(xs, xb, t0, T, rstd, tmpfb[0])
        for fc in range(16):
            op(DVE, lambda: nc.vector.tensor_tensor(out=xs(fc, t0, T), in0=xs(fc, t0, T), in1=rstd, op=ALU.mult),
               reads=[tmpfb[0]], writes=[(xb[fc], t0, t0 + T)])
            op(ACT, lambda: nc.scalar.activation(out=xs(fc, t0, T), in_=xs(fc, t0, T), func=AF.Identity,
                                                 scale=sm(S_FING + fc)),
               reads=[consts], writes=[(xb[fc], t0, t0 + T)])
        with nc.allow_non_contiguous_dma(reason="per-tile output store"):
            dma_load(SP, outb, d_out.rearrange("f p t -> p f t")[:, :, t0 - HALO:t0 - HALO + T],
                     x_t[:, :, t0 - XO:t0 - XO + T], reads=[(xb[fc_], t0, t0 + T) for fc_ in range(16)])
    SP.wait(outb.w)
    assert W.next == NU or n_layers < DEPTH, (W.next, NU)
    nc._used_units = W.next
    return nc, plan


def _pc(v, n):
    return np.ascontiguousarray(np.asarray(v, np.float32).reshape(n, 128).T)


def prep_inputs(inp, plan, cores=None):
    f32 = np.float32
    x = np.asarray(inp["x"], f32)[0]
    wstream = np.empty((len(plan), 128, UNIT), f32)
    for u, spec in enumerate(plan):
        fill_unit(spec, inp, wstream[u])
    sm = np.zeros((128, NS), f32)
    for l in range(DEPTH):
        sm[:, S_ADAB + l * 96:S_ADAB + (l + 1) * 96] = _pc(inp["ada_b"][l], 96)
        sm[:, S_GMIX + l * 16:S_GMIX + (l + 1) * 16] = _pc(inp["norm_mix_g"][l], 16)
        sm[:, S_GFFN + l * 16:S_GFFN + (l + 1) * 16] = _pc(inp["norm_ffn_g"][l], 16)
    sm[:, S_FING:S_FING + 16] = _pc(inp["final_g"], 16)
    for i in range(2):
        cwv = np.asarray(inp["a_conv_w"][i], f32)
        sm[:, S_CONVW + i * 8 * CONVW:S_CONVW + (i + 1) * 8 * CONVW] = \
            cwv.reshape(CONVW, 8, 128).transpose(2, 1, 0).reshape(128, 8 * CONVW)
        sm[:, S_CONVB + i * 8:S_CONVB + (i + 1) * 8] = _pc(inp["a_conv_b"][i], 8)
        sm[:, S_AG + i * 8:S_AG + (i + 1) * 8] = _pc(inp["a_norm_g"][i], 8)
        sm[:, S_AB + i * 8:S_AB + (i + 1) * 8] = _pc(inp["a_norm_b"][i], 8)
        sm[:, S_VG + i * 8:S_VG + (i + 1) * 8] = _pc(inp["b_norm_g"][i], 8)
        sm[:, S_VB + i * 8:S_VB + (i + 1) * 8] = _pc(inp["b_norm_b"][i], 8)
        sm[:, S_PSC + i * 16:S_PSC + (i + 1) * 16] = _pc(inp["pool_scale"][i], 16)
    sm[:, S_CVEC:S_CVEC + 16] = _pc(inp["c"][0], 16)
    ident = np.eye(128, dtype=f32)
    cmask = np.triu(np.ones((128, 128), f32))
    bbias = np.empty((2, 128, 1024), f32)
    wsT = np.empty((2, 128, 1024), f32)
    for i in range(2):
        bbias[i] = np.broadcast_to(np.asarray(inp["b_bias"][i], f32).reshape(1, 1024), (128, 1024))
        wsT[i] = np.asarray(inp["b_w_s"][i], f32).transpose(2, 0, 1).reshape(128, 1024)
    maps = []
    for k in (range(NCORE) if cores is None else cores):
        lo = k * OWN - HALO
        xt = np.zeros((NT, D), f32)
        if lo < 0:
            xt[-lo:] = x[0:lo + NT]
        else:
            xt[:] = x[lo:lo + NT]
        xT = np.ascontiguousarray(xt.T).reshape(16, 128, NT)
        hmask = np.full((128, 256), 0.0 if k == 0 else 1.0, f32)
        invc = np.empty((128, 64), f32)
        for g, w in enumerate(POOLW):
            for j in range(16):
                pos = k * OWN + j
                invc[:, g * 16 + j] = 1.0 / min(pos + 1, w)
        maps.append({"xT": xT, "wstream": wstream, "smalls": sm, "hmask": hmask, "invcnt": invc,
                     "ident": ident, "cmask": cmask, "bbias": bbias, "wsT": wsT})
    return maps


_CACHE = {}


def kernel(**inputs):
    inp = {k: np.asarray(v) for k, v in inputs.items()}
    if "prog" not in _CACHE:
        _CACHE["prog"] = build_program()
    nc, plan = _CACHE["prog"]
    maps = prep_inputs(inp, plan)
    res = run_bass_kernel_spmd(nc, maps, core_ids=list(range(NCORE)))
    outs = []
    for k in range(NCORE):
        o = np.asarray(res.results[k]["outT"], np.float32).reshape(D, OWN)
        outs.append(o.T)
    return np.ascontiguousarray(np.concatenate(outs, axis=0)[None].astype(np.float32))
```

```python
import numpy as np
import concourse.bass as bass
import concourse.mybir as mybir
from concourse.bass_utils import run_bass_kernel_spmd

F32 = mybir.dt.float32
BF16 = mybir.dt.bfloat16
AF = mybir.ActivationFunctionType
ALU = mybir.AluOpType
AX = mybir.AxisListType

D = 2048
KC = 16
NCORE = 8
OWN = 1024
HALO = 256
NT = OWN + HALO
XO = 112
NX = NT - XO
DFF = 5632
JC = DFF // 128
GS = 4
NG = JC // GS
DEPTH = 4
EPS = 1e-6
NSLOT = 8
WAIT_SLACK = 3
UNIT = 2048
CONVW = 31
POOLW = (2, 4, 8, 16)

MIX_H_START = [0, 112, 128, 240]
MIX_OUT_START = [112, 128, 240, 256]
A_BACK = 32

_off = 0
def _take(n):
    global _off
    o = _off
    _off += n
    return o
S_ADAB = _take(DEPTH * 96)
S_GMIX = _take(DEPTH * 16)
S_GFFN = _take(DEPTH * 16)
S_FING = _take(16)
S_CONVW = _take(2 * 8 * CONVW)
S_CONVB = _take(2 * 8)
S_AG = _take(2 * 8)
S_AB = _take(2 * 8)
S_VG = _take(2 * 8)
S_VB = _take(2 * 8)
S_PSC = _take(2 * 16)
S_CVEC = _take(16)
NS = _off


def tiles(s, e, maxt=512):
    n = -(-(e - s) // maxt)
    base = -(-(e - s) // n)
    base = -(-base // 8) * 8
    out = []
    t = s
    while t < e:
        T = min(base, e - t)
        out.append((t, T))
        t += T
    return out


NORM_FILL = 15


def ada_ffn_base(l):
    return NORM_FILL if l in (1, 2) else 0


def ada_split(step, first=0, nsteps=JC - GS):
    step -= GS
    if step < 0:
        return 0, 0
    n = 96 - first
    lo = first + -(-n * step // nsteps)
    hi = first + -(-n * (step + 1) // nsteps)
    return lo, hi


def unit_plan():
    plan = []
    for c in range(32):
        plan.append(("ada", 0, c))
    for l in range(DEPTH):
        i = l // 2
        if l % 2 == 0:
            if l == 2:
                for c in range(NORM_FILL):
                    plan.append(("ada", 3, c))
            for c in range(8):
                plan.append(("win", i, c))
                plan.append(("win", i, 8 + c))
                if l == 0:
                    for cc in range(32 + 8 * c, 40 + 8 * c):
                        plan.append(("ada", 0, cc))
            for fp in range(8):
                plan.append(("wout", i, 0, fp))
            for c in range(8):
                plan.append(("win", i, 16 + c))
            for half in range(2):
                for kg in range(4):
                    plan.append(("winv", i, half, kg))
            for fp in range(8):
                plan.append(("wout", i, 1, fp))
        else:
            if l + 1 < DEPTH:
                for c in range(NORM_FILL):
                    plan.append(("ada", l + 1, c))
            for g in range(4):
                plan.append(("pool", i, g))
        if l + 1 < DEPTH:
            for c in range(ada_ffn_base(l), ada_ffn_base(l) + NORM_FILL):
                plan.append(("ada", l + 1, c))
        for q in range(NG):
            for jj in range(GS):
                plan.append(("w1", l, q * GS + jj))
                plan.append(("w3", l, q * GS + jj))
                if l + 1 < DEPTH:
                    lo, hi = ada_split(q * GS + jj, ada_ffn_base(l) + NORM_FILL)
                    for c in range(lo, hi):
                        plan.append(("ada", l + 1, c))
            for jj in range(GS):
                plan.append(("w2", l, q * GS + jj))
    return plan


def _kmajor(a):
    return a.reshape(16, 128, 128).transpose(1, 0, 2).reshape(128, UNIT)


def fill_unit(spec, inp, out):
    kind = spec[0]
    if kind == "ada":
        _, l, c = spec
        out[...] = _kmajor(inp["ada_w"][l][:, c * 128:(c + 1) * 128])
    elif kind == "win":
        _, i, m = spec
        out[...] = _kmajor(inp["ab_w_in"][i][:, m * 128:(m + 1) * 128])
    elif kind == "winv":
        _, i, half, kg = spec
        w = inp["ab_w_in"][i][kg * 512:(kg + 1) * 512, 3072 + half * 512:3072 + (half + 1) * 512]
        out[...] = w.reshape(4, 128, 512).transpose(1, 0, 2).reshape(128, UNIT)
    elif kind == "wout":
        _, i, part, fp = spec
        w = inp["ab_w_out"][i][part * 1024:(part + 1) * 1024, fp * 256:(fp + 1) * 256]
        out[...] = w.reshape(8, 128, 2, 128).transpose(1, 2, 0, 3).reshape(128, UNIT)
    elif kind == "w1":
        _, l, j = spec
        out[...] = _kmajor(inp["ffn_w1"][l][:, j * 128:(j + 1) * 128])
    elif kind == "w3":
        _, l, j = spec
        out[...] = _kmajor(inp["ffn_w3"][l][:, j * 128:(j + 1) * 128])
    elif kind == "w2":
        _, l, j = spec
        out[...] = inp["ffn_w2"][l][j * 128:(j + 1) * 128, :]
    elif kind == "pool":
        _, i, g = spec
        out[...] = inp["pool_w"][i][g].reshape(4, 128, 512).transpose(1, 0, 2).reshape(128, UNIT)
    else:
        raise ValueError(spec)


class Tok:
    __slots__ = ("eng", "sem", "val", "key")

    def __init__(self, eng, sem, val, key):
        self.eng, self.sem, self.val, self.key = eng, sem, val, key


class Eng:
    def __init__(self, nc, raw, name):
        self.nc, self.raw, self.name = nc, raw, name
        self.sem = nc.alloc_semaphore("es_" + name)
        self.count = 0
        self.waited = {}

    def wait(self, tok):
        if tok is None or (tok.eng is self and self.name == "pe"):
            return
        if self.waited.get(tok.key, 0) >= tok.val:
            return
        val = tok.val
        if tok.eng is not None and tok.eng is not self:
            alt = tok.eng.count - WAIT_SLACK
            if alt > val:
                val = alt
        self.raw.wait_ge(tok.sem, val)
        self.waited[tok.key] = val

    def signal(self, ins):
        ins.then_inc(self.sem, 1)
        self.count += 1
        return Tok(self, self.sem, self.count, self.name)


class Buf:
    def __init__(self, name, nc=None, dma=False):
        self.name = name
        self.w = None
        self.r = {}
        if dma:
            self.dsem = nc.alloc_semaphore("ds_" + name)
            self.dcount = 0

    def deps(self, write, lo=None, hi=None):
        out = [self.w]
        if write:
            out.extend(self.r.values())
        return out

    def record(self, tok, write, ename, lo=None, hi=None):
        if write:
            self.w = tok
            self.r = {}
        else:
            self.r[ename] = tok


class RBuf:
    def __init__(self, name):
        self.name = name
        self.ent = []

    def deps(self, write, lo, hi):
        out = []
        for e in self.ent:
            if e[0] < hi and lo < e[1] and (write or e[3]):
                out.append(e[2])
        return out

    def record(self, tok, write, ename, lo, hi):
        if write:
            self.ent = [e for e in self.ent if not (lo <= e[0] and e[1] <= hi)]
        else:
            self.ent = [e for e in self.ent if not ((not e[3]) and e[4] == ename and lo <= e[0] and e[1] <= hi)]
        self.ent.append([lo, hi, tok, write, ename])

    def set_all(self, tok, lo, hi):
        self.ent = [[lo, hi, tok, True, "init"]]


def _items(lst):
    for it in lst:
        if isinstance(it, tuple):
            yield it
        else:
            yield (it, None, None)


def op(E, fn, reads=(), writes=()):
    rl = list(_items(reads))
    wl = list(_items(writes))
    for b, lo, hi in rl:
        for t in b.deps(False, lo, hi):
            E.wait(t)
    for b, lo, hi in wl:
        for t in b.deps(True, lo, hi):
            E.wait(t)
    ins = fn()
    tok = E.signal(ins)
    for b, lo, hi in rl:
        b.record(tok, False, E.name, lo, hi)
    for b, lo, hi in wl:
        b.record(tok, True, E.name, lo, hi)
    return ins


def dma_load(Q, buf, out_ap, in_ap, reads=()):
    rl = list(_items(reads))
    for b, lo, hi in rl:
        for t in b.deps(False, lo, hi):
            Q.wait(t)
    if buf.w is not None and buf.w.key != "d_" + buf.name:
        Q.wait(buf.w)
    for t in buf.r.values():
        Q.wait(t)
    ins = Q.raw.dma_start(out=out_ap, in_=in_ap)
    ins.then_inc(buf.dsem, 16)
    buf.dcount += 16
    tok = Tok(None, buf.dsem, buf.dcount, "d_" + buf.name)
    buf.w = tok
    buf.r = {}
    for b, lo, hi in rl:
        b.record(tok, False, "d_" + buf.name, lo, hi)
    return tok


def build_program(n_layers=DEPTH, dbg=False, plan_len=None, stage=99):
    stage_in = stage
    nc = bass.Bass("TRN2", target_bir_lowering=False)
    plan = unit_plan()[:plan_len]
    NU = len(plan)

    d_x = nc.dram_tensor("xT", [16, 128, NT], F32, kind="ExternalInput").ap()
    d_w = nc.dram_tensor("wstream", [NU, 128, UNIT], F32, kind="ExternalInput").ap()
    d_sm = nc.dram_tensor("smalls", [128, NS], F32, kind="ExternalInput").ap()
    d_hmask = nc.dram_tensor("hmask", [128, 256], F32, kind="ExternalInput").ap()
    d_invc = nc.dram_tensor("invcnt", [128, 64], F32, kind="ExternalInput").ap()
    d_ident = nc.dram_tensor("ident", [128, 128], F32, kind="ExternalInput").ap()
    d_cmask = nc.dram_tensor("cmask", [128, 128], F32, kind="ExternalInput").ap()
    d_bbias = nc.dram_tensor("bbias", [2, 128, 1024], F32, kind="ExternalInput").ap()
    d_wsT = nc.dram_tensor("wsT", [2, 128, 1024], F32, kind="ExternalInput").ap()
    d_out = nc.dram_tensor("outT", [16, 128, OWN], F32, kind="ExternalOutput").ap()
    if dbg:
        d_dbg = nc.dram_tensor("dbg", [16, 128, NX], F32, kind="ExternalOutput").ap()

    x_t = nc.alloc_sbuf_tensor("x_t", [128, 16, NX], F32)
    hb_t = nc.alloc_sbuf_tensor("hb_t", [128, 16, NT], BF16)
    ring_t = nc.alloc_sbuf_tensor("ring_t", [128, NSLOT, UNIT], BF16)
    SCR_BYTES = 52224
    scr_t = nc.alloc_sbuf_tensor("scr_t", [128, SCR_BYTES // 2], BF16)
    sm_t = nc.alloc_sbuf_tensor("sm_t", [128, NS], F32)
    hmask_t = nc.alloc_sbuf_tensor("hmask_t", [128, 256], F32)
    invc_t = nc.alloc_sbuf_tensor("invc_t", [128, 64], F32)
    ident_t = nc.alloc_sbuf_tensor("ident_t", [128, 128], F32)
    cmask_t = nc.alloc_sbuf_tensor("cmask_t", [128, 128], F32)
    ones_t = nc.alloc_sbuf_tensor("ones_t", [128, 128], BF16)
    identb_t = nc.alloc_sbuf_tensor("identb_t", [128, 128], BF16)
    cond_t = nc.alloc_sbuf_tensor("cond_t", [128, 16], BF16)
    mod_t = nc.alloc_sbuf_tensor("mod_t", [128, 2, 96], F32)
    vec_t = nc.alloc_sbuf_tensor("vec_t", [128, 2, 4, 16], F32)
    st_t = nc.alloc_sbuf_tensor("st_t", [128, 16], F32)
    eps_t = nc.alloc_sbuf_tensor("eps_t", [128, 1], F32)

    scr = scr_t[:]

    def carve(off_bytes, nbytes, dtype, shape):
        a = scr[:, off_bytes // 2:(off_bytes + nbytes) // 2]
        if dtype == F32:
            a = a.bitcast(F32)
        if shape is None:
            return a
        if len(shape) == 2:
            return a.rearrange("p (a b) -> p a b", a=shape[0])
        if len(shape) == 3:
            return a.rearrange("p (a b c) -> p a b c", a=shape[0], b=shape[1])
        return a

    o = 0
    CAT_B = 8 * NX * 2
    cat = carve(o, CAT_B, BF16, (8, NX)); o += CAT_B
    gbuf = carve(0, GS * NX * 2, BF16, (GS, NX))
    NTA = NT - (MIX_OUT_START[0] - A_BACK)
    apad = carve(o, 2 * NTA * 2, BF16, (2, NTA)); o += 2 * NTA * 2
    diag = carve(o, CONVW * 128 * 2, BF16, (CONVW, 128)); o += CONVW * 128 * 2
    T1 = carve(o, 4096, F32, (8, 128)); o += 4096
    wct = carve(o, 2048, BF16, (8, 128)); o += 2048
    vT = carve(o, 4096, BF16, (2, 1024)); o += 4096
    tmpf = carve(o, 4 * 2048, F32, (4, 512)); o += 4 * 2048
    tmpb = carve(o, 2 * 1024, BF16, (2, 512)); o += 2 * 1024
    assert o <= SCR_BYTES, o
    NH = NT - MIX_H_START[1]
    po = 0
    p_rstd = carve(po, NH * 4, F32, None); po += NH * 4
    psets = []
    for _si in range(2):
        st_ = []
        for _k in range(3):
            st_.append(carve(po, NH * 4, F32, None)); po += NH * 4
        psets.append(st_)
    assert po <= o - 4 * 2048 - 2 * 1024, (po, o)
    rstd_all = carve(9344, NT * 4, F32, None)
    xh0 = carve(0, 16 * XO * 4, F32, (16, XO))
    wstmp = carve(CAT_B, 4096, F32, (8, 128))

    P = [nc.alloc_psum_tensor(f"ps{i}", [128, 512], F32) for i in range(8)]
    PB = [Buf(f"ps{i}") for i in range(8)]

    PE = Eng(nc, nc.tensor, "pe")
    ACT = Eng(nc, nc.scalar, "act")
    DVE = Eng(nc, nc.vector, "dve")
    POOL = Eng(nc, nc.gpsimd, "pool")
    SP = Eng(nc, nc.sync, "sp")

    xb = [RBuf(f"x{f}") for f in range(16)]
    hbb = [Buf(f"hb{f}") for f in range(16)]
    hbt = [Buf(f"hbt{i}") for i in range(4)]
    slotb = [Buf(f"slot{s}", nc, dma=True) for s in range(NSLOT)]
    xload = Buf("xload", nc, dma=True)
    consts = Buf("consts", nc, dma=True)
    catb = [RBuf(f"cat{c}") for c in range(8)]
    apadb = [Buf("apad0"), Buf("apad1")]
    diagb = Buf("diag")
    diagb2 = Buf("diag2")
    t1b = Buf("T1", nc, dma=True)
    wctb = Buf("wct")
    wstb = Buf("wstmp", nc, dma=True)
    vTb = [Buf("vT0"), Buf("vT1")]
    tmpfb = [Buf(f"tmpf{i}") for i in range(4)]
    tmpbb = [Buf(f"tmpb{i}") for i in range(2)]
    modb = [Buf("mod0"), Buf("mod1")]
    vecb = [Buf("vec0"), Buf("vec1")]
    stb = Buf("st")
    condb = Buf("cond")
    miscb = Buf("misc")
    poolb = Buf("poolscr")
    rstdallb = Buf("rstdall")
    psetb = [Buf("pset0"), Buf("pset1")]
    outb = Buf("outst", nc, dma=True)

    class Ring:
        def __init__(self):
            self.issued = 0
            self.next = 0
            self.released = [False] * NU

        def pump(self):
            while self.issued < NU and (self.issued < NSLOT or self.released[self.issued - NSLOT]):
                u = self.issued
                s = u % NSLOT
                dma_load(POOL, slotb[s], ring_t[:, s, :], d_w[u])
                self.issued += 1

        def get(self, spec):
            u = self.next
            assert plan[u] == spec, (u, plan[u], spec)
            self.pump()
            assert self.issued > u, ("ring stall", u, spec)
            self.next += 1
            return u

        def release(self, u):
            self.released[u] = True
            self.pump()

    W = Ring()

    def slot_ap(u):
        return ring_t[:, u % NSLOT, :]

    def slot_buf(u):
        return slotb[u % NSLOT]

    def xs(fc, t0, T):
        return x_t[:, fc, t0 - XO:t0 - XO + T]

    def hs_(fc, t0, T):
        return hb_t[:, fc, t0:t0 + T]

    def sm(off, n=1):
        return sm_t[:, off:off + n]

    def mm_group(ps_i, T, pairs, reads, Mrows=128, col0=0, rec_only=()):
        rl = list(_items(reads))
        for b, lo, hi in rl:
            for t in b.deps(False, lo, hi):
                PE.wait(t)
        pb = PB[ps_i]
        for t in pb.deps(True):
            PE.wait(t)
        n = len(pairs)
        ins = None
        for idx, (l_ap, r_ap) in enumerate(pairs):
            ins = nc.tensor.matmul(P[ps_i][:Mrows, col0:col0 + T], l_ap, r_ap,
                                   start=(idx == 0), stop=(idx == n - 1))
        tok = PE.signal(ins)
        for b, lo, hi in rl:
            b.record(tok, False, "pe", lo, hi)
        for b, lo, hi in _items(rec_only):
            b.record(tok, False, "pe", lo, hi)
        pb.record(tok, True, "pe")
        return tok

    for fc in range(16):
        dma_load(SP, xload, x_t[:, fc, :], d_x[fc][:, XO:])
    for fc in range(16):
        dma_load(SP, consts, xh0[:, fc, :], d_x[fc][:, 0:XO])
    dma_load(SP, consts, sm_t[:], d_sm)
    dma_load(SP, consts, hmask_t[:], d_hmask)
    dma_load(SP, consts, invc_t[:], d_invc)
    dma_load(SP, consts, ident_t[:], d_ident)
    dma_load(SP, consts, cmask_t[:], d_cmask)
    for f in range(16):
        xb[f].set_all(xload.w, 0, NT)
    op(DVE, lambda: nc.vector.memset(ones_t[:], 1.0), writes=[miscb])
    op(DVE, lambda: nc.vector.memset(eps_t[:], EPS), writes=[miscb])
    op(DVE, lambda: nc.vector.tensor_copy(out=identb_t[:], in_=ident_t[:]), reads=[consts], writes=[miscb])
    op(ACT, lambda: nc.scalar.activation(out=cond_t[:], in_=sm(S_CVEC, 16), func=AF.Silu),
       reads=[consts], writes=[condb])
    W.pump()

    def ada_chunks(l, c_lo, c_hi, bank=7, evac=True):
        par = l % 2
        if c_hi <= c_lo:
            return
        for c in range(c_lo, c_hi):
            u = W.get(("ada", l, c))
            ua = slot_ap(u).rearrange("p (k m) -> p k m", k=16)
            pairs = [(ua[:, k, :], cond_t[:, k:k + 1]) for k in range(16)]
            for b in (slot_buf(u), condb):
                PE.wait(b.w)
            pb = PB[bank]
            if c == c_lo:
                PE.wait(pb.w)
                for t in pb.r.values():
                    PE.wait(t)
            ins = None
            for k in range(16):
                ins = nc.tensor.matmul(P[bank][:, c:c + 1], pairs[k][0], pairs[k][1],
                                       start=(k == 0), stop=(k == 15))
            tok = PE.signal(ins)
            slot_buf(u).r["pe"] = tok
            condb.r["pe"] = tok
            pb.w = tok
            pb.r = {}
            W.release(u)
        if evac:
            ada_evac(l, c_lo, c_hi, bank)

    def ada_evac(l, c_lo, c_hi, bank):
        par = l % 2
        op(DVE, lambda: nc.vector.tensor_tensor(out=mod_t[:, par, c_lo:c_hi], in0=P[bank][:, c_lo:c_hi],
                                                in1=sm(S_ADAB + l * 96 + c_lo, c_hi - c_lo), op=ALU.add),
           reads=[PB[bank], consts], writes=[modb[par]])

    def ada_vectors(l, part=None):
        par = l % 2
        i = l // 2
        if part == 'b':
            return ada_vectors_b(l)
        op(DVE, lambda: nc.vector.scalar_tensor_tensor(out=vec_t[:, par, 0, :], in0=mod_t[:, par, 16:32], scalar=1.0,
                                                       in1=sm(S_GMIX + l * 16, 16), op0=ALU.add, op1=ALU.mult),
           reads=[modb[par], consts], writes=[vecb[par]])
        if part == 'a':
            return
        ada_vectors_b(l)

    def ada_vectors_b(l):
        par = l % 2
        i = l // 2
        if l % 2 == 1:
            op(DVE, lambda: nc.vector.tensor_tensor(out=vec_t[:, par, 1, :], in0=mod_t[:, par, 32:48],
                                                    in1=sm(S_PSC + i * 16, 16), op=ALU.mult),
               reads=[modb[par], consts], writes=[vecb[par]])
        else:
            op(DVE, lambda: nc.vector.tensor_copy(out=vec_t[:, par, 1, :], in_=mod_t[:, par, 32:48]),
               reads=[modb[par]], writes=[vecb[par]])
        op(DVE, lambda: nc.vector.scalar_tensor_tensor(out=vec_t[:, par, 2, :], in0=mod_t[:, par, 64:80], scalar=1.0,
                                                       in1=sm(S_GFFN + l * 16, 16), op0=ALU.add, op1=ALU.mult),
           reads=[modb[par], consts], writes=[vecb[par]])

    def A1(l, f): return vec_t[:, l % 2, 0, f:f + 1]
    def B1(l, f): return mod_t[:, l % 2, f:f + 1]
    def G1(l, f): return vec_t[:, l % 2, 1, f:f + 1]
    def A2(l, f): return vec_t[:, l % 2, 2, f:f + 1]
    def B2(l, f): return mod_t[:, l % 2, 48 + f:48 + f + 1]
    def G2(l, f): return mod_t[:, l % 2, 80 + f:80 + f + 1]

    def rms_rstd(src, src_bufs, t0, T, dst_ap, dst_buf):
        for fc in range(16):
            sq = hs_(fc, t0, T)
            sqb = hbb[fc]
            s_ap = src(fc, t0, T)
            if fc % 2 == 0:
                op(ACT, lambda: nc.scalar.activation(out=sq, in_=s_ap, func=AF.Square),
                   reads=[(src_bufs[fc], t0, t0 + T)], writes=[sqb])
            else:
                op(DVE, lambda: nc.vector.tensor_tensor(out=sq, in0=s_ap, in1=s_ap, op=ALU.mult),
                   reads=[(src_bufs[fc], t0, t0 + T)], writes=[sqb])
        for fc in range(16):
            sq = hs_(fc, t0, T)
            sqb = hbb[fc]
            PE.wait(sqb.w)
            PE.wait(miscb.w)
            if fc == 0:
                PE.wait(PB[6].w)
                for t in PB[6].r.values():
                    PE.wait(t)
            ins = nc.tensor.matmul(P[6][:, :T], ones_t[:], sq, start=(fc == 0), stop=(fc == 15))
            if fc == 15:
                tok = PE.signal(ins)
                for b_ in hbb:
                    b_.r["pe"] = tok
                PB[6].w = tok
                PB[6].r = {}
        op(ACT, lambda: nc.scalar.activation(out=dst_ap, in_=P[6][:, :T], func=AF.Sqrt,
                                             bias=eps_t[:, 0:1], scale=1.0 / D),
           reads=[PB[6], miscb], writes=[dst_buf])
        op(DVE, lambda: nc.vector.reciprocal(out=dst_ap, in_=dst_ap), reads=[dst_buf], writes=[dst_buf])

    def norm_phase(tile_list, src, src_bufs, Afn, Bfn, extra_reads, pre=None, after_stats=None):
        for ti_, (t0, T) in enumerate(tile_list):
            if pre is None:
                rstd = tmpf[:, 0, :T]
                rstd_buf = tmpfb[0]
                rms_rstd(src, src_bufs, t0, T, rstd, rstd_buf)
                if after_stats is not None:
                    after_stats(ti_)
            else:
                rstd = pre[0][:, t0:t0 + T]
                rstd_buf = pre[1]
            for fc in range(16):
                tb = 1 + fc % 2
                tt = tmpf[:, tb, :T]
                s_ap = src(fc, t0, T)
                op(DVE, lambda: nc.vector.tensor_tensor(out=tt, in0=s_ap, in1=rstd, op=ALU.mult),
                   reads=[(src_bufs[fc], t0, t0 + T), rstd_buf], writes=[tmpfb[tb]])
                op(ACT, lambda: nc.scalar.activation(out=hs_(fc, t0, T), in_=tt, func=AF.Identity,
                                                     bias=Bfn(fc), scale=Afn(fc)),
                   reads=[tmpfb[tb]] + extra_reads, writes=[hbb[fc]])
            hbt[ti_].w = Tok(ACT, ACT.sem, ACT.count, ACT.name)
            hbt[ti_].r = {}

    def x_update(f, t0, T, ps_i, gate_ap, extra_reads):
        op(DVE, lambda: nc.vector.scalar_tensor_tensor(out=xs(f, t0, T), in0=P[ps_i][:, :T], scalar=gate_ap,
                                                       in1=xs(f, t0, T), op0=ALU.mult, op1=ALU.add),
           reads=[PB[ps_i]] + extra_reads, writes=[(xb[f], t0, t0 + T)])

    def ffn_phase(l):
        par = l % 2
        s = MIX_OUT_START[l]
        tl = tiles(s, NT)
        if l + 1 < DEPTH:
            assert len(tl) == 3
            ab0 = ada_ffn_base(l)
            norm_phase(tl, xs, xb, lambda f: A2(l, f), lambda f: B2(l, f), [vecb[par], modb[par]],
                       after_stats=lambda ti: ada_chunks(l + 1, ab0 + 5 * ti, ab0 + 5 * ti + 5, bank=7, evac=False))
            ada_evac(l + 1, ab0, ab0 + NORM_FILL, 7)
        else:
            norm_phase(tl, xs, xb, lambda f: A2(l, f), lambda f: B2(l, f), [vecb[par], modb[par]])
        gbufs = catb[:GS]
        ada_c = 0
        pp = 0
        def a_step(jj, u1, u3, t0, T, hreads, rec):
            nonlocal pp
            a1 = slot_ap(u1).rearrange("p (k m) -> p k m", k=16)
            a3 = slot_ap(u3).rearrange("p (k m) -> p k m", k=16)
            pa, pb_ = (0, 1) if pp % 2 == 0 else (2, 3)
            pp += 1
            mm_group(pa, T, [(a1[:, k, :], hs_(k, t0, T)) for k in range(16)], [slot_buf(u1)] + hreads, rec_only=rec)
            mm_group(pb_, T, [(a3[:, k, :], hs_(k, t0, T)) for k in range(16)], [slot_buf(u3)] + hreads, rec_only=rec)
            tb = 2 + (pp % 2)
            sg = tmpf[:, tb, :T]
            op(ACT, lambda: nc.scalar.activation(out=sg, in_=P[pa][:, :T], func=AF.Silu),
               reads=[PB[pa]], writes=[tmpfb[tb]])
            op(DVE, lambda: nc.vector.tensor_tensor(out=gbuf[:, jj, t0 - XO:t0 - XO + T], in0=sg,
                                                    in1=P[pb_][:, :T], op=ALU.mult),
               reads=[tmpfb[tb], PB[pb_]], writes=[(gbufs[jj], t0, t0 + T)])

        for q in range(NG):
            if q == 0:
                us0 = []
                for jj in range(GS):
                    us0.append((W.get(("w1", l, jj)), W.get(("w3", l, jj))))
                for ti, (t0, T) in enumerate(tl):
                    for jj in range(GS):
                        u1, u3 = us0[jj]
                        a_step(jj, u1, u3, t0, T, [hbt[ti]], hbb)
                        if ti == len(tl) - 1:
                            W.release(u1)
                            W.release(u3)
            else:
                for jj in range(GS):
                    j = q * GS + jj
                    u1 = W.get(("w1", l, j))
                    u3 = W.get(("w3", l, j))
                    for (t0, T) in tl:
                        a_step(jj, u1, u3, t0, T, hbb, ())
                    W.release(u1)
                    W.release(u3)
                    if l + 1 < DEPTH:
                        alo, ahi = ada_split(j, ada_ffn_base(l) + NORM_FILL)
                        ada_chunks(l + 1, alo, ahi, bank=6 + j % 2)
            us = [W.get(("w2", l, q * GS + jj)) for jj in range(GS)]
            pc = 0
            for f in range(16):
                for (t0, T) in tl:
                    pi = (4, 5, 0, 1, 2, 3)[pc % 6]
                    pc += 1
                    mm_group(pi, T, [(slot_ap(us[jj])[:, f * 128:(f + 1) * 128], gbuf[:, jj, t0 - XO:t0 - XO + T])
                                     for jj in range(GS)],
                             [slot_buf(u) for u in us] + [(gb_, t0, t0 + T) for gb_ in gbufs])
                    x_update(f, t0, T, pi, G2(l, f), [modb[par]])
            for u in us:
                W.release(u)
        if l + 1 < DEPTH:
            ada_vectors(l + 1)

    def mixer_ab(l):
        stage = stage_in if l == n_layers - 1 else 99
        i = l // 2
        par = l % 2
        hs0 = MIX_H_START[l]
        so = MIX_OUT_START[l]
        a0 = so - A_BACK
        nta = NT - a0
        cw = S_CONVW + i * 8 * CONVW

        if l == 0:
            norm_phase([(0, XO)], src0, [consts] * 16, lambda f: A1(l, f), lambda f: B1(l, f), [vecb[par], modb[par]],
                       pre=(rstd_all, rstdallb))
            norm_phase(tiles(XO, NT), xs, xb, lambda f: A1(l, f), lambda f: B1(l, f), [vecb[par], modb[par]],
                       pre=(rstd_all, rstdallb))
        else:
            ntl = tiles(hs0, NT)
            if l == 2 and DEPTH > 3:
                assert len(ntl) == 3
                norm_phase(ntl, xs, xb, lambda f: A1(l, f), lambda f: B1(l, f), [vecb[par], modb[par]],
                           after_stats=lambda ti: ada_chunks(3, 5 * ti, 5 * ti + 5, bank=7, evac=False))
                ada_evac(3, 0, NORM_FILL, 7)
            else:
                norm_phase(ntl, xs, xb, lambda f: A1(l, f), lambda f: B1(l, f), [vecb[par], modb[par]])

        if stage <= 1:
            return
        for e_ in (PE, ACT, DVE):
            if e_.count > 0:
                SP.wait(Tok(e_, e_.sem, e_.count, e_.name))
        dma_load(SP, wstb, wstmp.rearrange("p a b -> p (a b)"), d_wsT[i])
        dma_load(SP, t1b, T1.rearrange("p a b -> p (a b)"), d_bbias[i])
        for h in range(8):
            op(DVE, lambda: nc.vector.tensor_tensor(out=wct[:, h, :], in0=wstmp[:, h, :], in1=cmask_t[:], op=ALU.mult),
               reads=[wstb, consts], writes=[wctb])
        wflat = wct.rearrange("p a b -> p (a b)")
        mm_group(0, 512, [(ones_t[:], wflat[:, 0:512])], [wctb, miscb])
        mm_group(1, 512, [(ones_t[:], wflat[:, 512:1024])], [wctb, miscb])
        for h in range(8):
            pi = h // 4
            op(DVE, lambda: nc.vector.scalar_tensor_tensor(out=T1[:, h, :], in0=P[pi][:, (h % 4) * 128:(h % 4 + 1) * 128],
                                                           scalar=sm(S_VB + i * 8 + h), in1=T1[:, h, :],
                                                           op0=ALU.mult, op1=ALU.add),
               reads=[PB[pi], t1b, consts], writes=[t1b])

        tla = tiles(a0, NT)
        tlo = tiles(so, NT)
        pp = 0

        def conv(c):
            nonlocal pp
            for (t0, T) in tlo:
                pi = 4 + pp % 2
                pp += 1
                base = (t0 - so) + 2
                mm_group(pi, T, [(diag[:, j, :], apad[:, c % 2, base + j:base + j + T]) for j in range(CONVW)],
                         [diagb, diagb2, apadb[c % 2]])
                op(ACT, lambda: nc.scalar.activation(out=cat[:, c, t0 - XO:t0 - XO + T], in_=P[pi][:, :T],
                                                     func=AF.Identity, bias=sm(S_CONVB + i * 8 + c), scale=1.0),
                   reads=[PB[pi], consts], writes=[(catb[c], t0, t0 + T)])

        def build_diag(c):
            for j in range(CONVW):
                if j % 2 == 0:
                    op(ACT, lambda: nc.scalar.activation(out=diag[:, j, :], in_=ident_t[:], func=AF.Identity,
                                                         scale=sm(cw + c * CONVW + j)),
                       reads=[consts], writes=[diagb])
                else:
                    op(DVE, lambda: nc.vector.tensor_scalar(out=diag[:, j, :], in0=ident_t[:],
                                                            scalar1=sm(cw + c * CONVW + j), scalar2=None, op0=ALU.mult),
                       reads=[consts], writes=[diagb2])

        pq = 0
        for c in range(8):
            uv = W.get(("win", i, c))
            ug = W.get(("win", i, 8 + c))
            av = slot_ap(uv).rearrange("p (k m) -> p k m", k=16)
            ag = slot_ap(ug).rearrange("p (k m) -> p k m", k=16)
            for (t0, T) in tla:
                pa, pb_ = (0, 1) if pq % 2 == 0 else (2, 3)
                pq += 1
                mm_group(pa, T, [(av[:, k, :], hs_(k, t0, T)) for k in range(16)], [slot_buf(uv)] + hbb)
                mm_group(pb_, T, [(ag[:, k, :], hs_(k, t0, T)) for k in range(16)], [slot_buf(ug)] + hbb)
                tb = 2 + (pq % 2)
                sg = tmpf[:, tb, :T]
                op(ACT, lambda: nc.scalar.activation(out=sg, in_=P[pb_][:, :T], func=AF.Sigmoid),
                   reads=[PB[pb_]], writes=[tmpfb[tb]])
                op(DVE, lambda: nc.vector.tensor_tensor(out=apad[:, c % 2, t0 - a0:t0 - a0 + T], in0=sg,
                                                        in1=P[pa][:, :T], op=ALU.mult),
                   reads=[tmpfb[tb], PB[pa]], writes=[apadb[c % 2]])
            nh = HALO - a0
            op(DVE, lambda: nc.vector.tensor_tensor(out=apad[:, c % 2, 0:nh], in0=apad[:, c % 2, 0:nh],
                                                    in1=hmask_t[:, 0:nh], op=ALU.mult),
               reads=[consts], writes=[apadb[c % 2]])
            W.release(uv)
            W.release(ug)
            if l == 0:
                ada_chunks(0, 32 + 8 * c, 40 + 8 * c, bank=6 + c % 2)
            if c >= 1:
                conv(c - 1)
            build_diag(c)
        conv(7)
        if l == 0:
            ada_vectors(0, 'b')

        if stage <= 2:
            return
        for (t0, T) in tlo:
            for c in range(8):
                sq = tmpb[:, c % 2, :T]
                sqb = tmpbb[c % 2]
                c_ap = cat[:, c, t0 - XO:t0 - XO + T]
                if c % 2 == 0:
                    op(ACT, lambda: nc.scalar.activation(out=sq, in_=c_ap, func=AF.Square),
                       reads=[(catb[c], t0, t0 + T)], writes=[sqb])
                else:
                    op(DVE, lambda: nc.vector.tensor_tensor(out=sq, in0=c_ap, in1=c_ap, op=ALU.mult),
                       reads=[(catb[c], t0, t0 + T)], writes=[sqb])
                for b in (sqb, miscb):
                    PE.wait(b.w)
                for t in catb[c].deps(False, t0, t0 + T):
                    PE.wait(t)
                if c == 0:
                    for pi in (6, 7):
                        PE.wait(PB[pi].w)
                        for t in PB[pi].r.values():
                            PE.wait(t)
                nc.tensor.matmul(P[6][:, :T], ones_t[:], c_ap, start=(c == 0), stop=(c == 7))
                ins = nc.tensor.matmul(P[7][:, :T], ones_t[:], sq, start=(c == 0), stop=(c == 7))
                tok = PE.signal(ins)
                sqb.r["pe"] = tok
                catb[c].record(tok, False, "pe", t0, t0 + T)
                for pi in (6, 7):
                    PB[pi].w = tok
                    PB[pi].r = {}
            mu = tmpf[:, 0, :T]
            tB = tmpf[:, 1, :T]
            op(DVE, lambda: nc.vector.tensor_scalar(out=mu, in0=P[6][:, :T], scalar1=1.0 / 1024, scalar2=None, op0=ALU.mult),
               reads=[PB[6]], writes=[tmpfb[0]])
            op(DVE, lambda: nc.vector.tensor_tensor(out=tB, in0=mu, in1=mu, op=ALU.mult),
               reads=[tmpfb[0]], writes=[tmpfb[1]])
            op(DVE, lambda: nc.vector.scalar_tensor_tensor(out=tB, in0=P[7][:, :T], scalar=1.0 / 1024, in1=tB,
                                                           op0=ALU.mult, op1=ALU.subtract),
               reads=[PB[7]], writes=[tmpfb[1]])
            op(ACT, lambda: nc.scalar.activation(out=tB, in_=tB, func=AF.Sqrt, bias=eps_t[:, 0:1], scale=1.0),
               reads=[tmpfb[1], miscb], writes=[tmpfb[1]])
            op(DVE, lambda: nc.vector.reciprocal(out=tB, in_=tB), reads=[tmpfb[1]], writes=[tmpfb[1]])
            op(DVE, lambda: nc.vector.tensor_tensor(out=mu, in0=mu, in1=tB, op=ALU.mult),
               reads=[tmpfb[1]], writes=[tmpfb[0]])
            for c in range(8):
                tb = 2 + c % 2
                u_ap = tmpf[:, tb, :T]
                c_ap = cat[:, c, t0 - XO:t0 - XO + T]
                op(DVE, lambda: nc.vector.tensor_tensor(out=u_ap, in0=c_ap, in1=tB, op=ALU.mult),
                   reads=[(catb[c], t0, t0 + T), tmpfb[1]], writes=[tmpfb[tb]])
                op(DVE, lambda: nc.vector.tensor_tensor(out=u_ap, in0=u_ap, in1=mu, op=ALU.subtract),
                   reads=[tmpfb[0]], writes=[tmpfb[tb]])
                op(ACT, lambda: nc.scalar.activation(out=c_ap, in_=u_ap, func=AF.Silu,
                                                     bias=sm(S_AB + i * 8 + c), scale=sm(S_AG + i * 8 + c)),
                   reads=[tmpfb[tb], consts], writes=[(catb[c], t0, t0 + T)])

        if stage <= 3:
            return
        def wout_part(part):
            pc = 0
            for fp in range(8):
                u = W.get(("wout", i, part, fp))
                ua = slot_ap(u).rearrange("p (f c m) -> p f c m", f=2, c=8)
                for ff in range(2):
                    f = 2 * fp + ff
                    for (t0, T) in tlo:
                        pi = (4, 5, 0, 1, 2, 3)[pc % 6]
                        pc += 1
                        mm_group(pi, T, [(ua[:, ff, c, :], cat[:, c, t0 - XO:t0 - XO + T]) for c in range(8)],
                                 [slot_buf(u)] + [(cb_, t0, t0 + T) for cb_ in catb])
                        x_update(f, t0, T, pi, G1(l, f), [vecb[par]])
                W.release(u)

        wout_part(0)
        if stage <= 4:
            return

        pq = 0
        for c in range(8):
            u = W.get(("win", i, 16 + c))
            ua = slot_ap(u).rearrange("p (k m) -> p k m", k=16)
            for (t0, T) in tlo:
                pa = pq % 2
                pq += 1
                mm_group(pa, T, [(ua[:, k, :], hs_(k, t0, T)) for k in range(16)], [slot_buf(u)] + hbb)
                op(ACT, lambda: nc.scalar.activation(out=cat[:, c, t0 - XO:t0 - XO + T], in_=P[pa][:, :T], func=AF.Identity),
                   reads=[PB[pa]], writes=[(catb[c], t0, t0 + T)])
            W.release(u)

        if stage <= 5:
            return
        uvs = [[W.get(("winv", i, half, kg)) for kg in range(4)] for half in range(2)]
        n0 = hs0 // 128
        NCH = NT // 128

        def bank_of(n, half):
            return (0, 1)[half] if (n - n0) % 2 == 0 else (4, 5)[half]

        def bv_proj(n):
            tk = n * 128
            for half in range(2):
                pairs = []
                for k in range(16):
                    ua = slot_ap(uvs[half][k // 4]).rearrange("p (k m) -> p k m", k=4)
                    pairs.append((hb_t[:, k, tk:tk + 128], ua[:, k % 4, :]))
                mm_group(bank_of(n, half), 512, pairs, [slot_buf(u) for u in uvs[half]] + hbb)

        bv_proj(n0)
        for n in range(n0, NCH):
            tk = n * 128
            if n + 1 < NCH:
                bv_proj(n + 1)
            op(DVE, lambda: nc.vector.memset(st_t[:, 0:4], 0.0), writes=[stb])
            for half in range(2):
                bk = bank_of(n, half)
                junk = tmpf[:, 2 + half, :]
                op(ACT, lambda: nc.scalar.activation(out=junk, in_=P[bk][:, :], func=AF.Identity,
                                                     accum_out=st_t[:, half:half + 1]),
                   writes=[PB[bk], tmpfb[2 + half], stb])
                op(ACT, lambda: nc.scalar.activation(out=junk, in_=P[bk][:, :], func=AF.Square,
                                                     accum_out=st_t[:, 2 + half:3 + half]),
                   writes=[PB[bk], tmpfb[2 + half], stb])
            op(DVE, lambda: nc.vector.tensor_tensor(out=st_t[:, 4:5], in0=st_t[:, 0:1], in1=st_t[:, 1:2], op=ALU.add), writes=[stb])
            op(DVE, lambda: nc.vector.tensor_tensor(out=st_t[:, 5:6], in0=st_t[:, 2:3], in1=st_t[:, 3:4], op=ALU.add), writes=[stb])
            op(DVE, lambda: nc.vector.tensor_scalar(out=st_t[:, 4:6], in0=st_t[:, 4:6], scalar1=1.0 / 1024, scalar2=None, op0=ALU.mult), writes=[stb])
            op(DVE, lambda: nc.vector.tensor_tensor(out=st_t[:, 6:7], in0=st_t[:, 4:5], in1=st_t[:, 4:5], op=ALU.mult), writes=[stb])
            op(DVE, lambda: nc.vector.tensor_tensor(out=st_t[:, 6:7], in0=st_t[:, 5:6], in1=st_t[:, 6:7], op=ALU.subtract), writes=[stb])
            op(ACT, lambda: nc.scalar.activation(out=st_t[:, 7:8], in_=st_t[:, 6:7], func=AF.Sqrt, bias=eps_t[:, 0:1], scale=1.0),
               reads=[stb, miscb], writes=[stb])
            op(DVE, lambda: nc.vector.reciprocal(out=st_t[:, 7:8], in_=st_t[:, 7:8]), reads=[stb], writes=[stb])
            op(DVE, lambda: nc.vector.scalar_tensor_tensor(out=st_t[:, 8:9], in0=st_t[:, 4:5], scalar=-1.0, in1=st_t[:, 7:8],
                                                           op0=ALU.mult, op1=ALU.mult), writes=[stb])
            vb = vTb[n % 2]
            for half in range(2):
                bk = bank_of(n, half)
                op(ACT, lambda: nc.scalar.activation(out=vT[:, n % 2, half * 512:(half + 1) * 512], in_=P[bk][:, :],
                                                     func=AF.Identity, bias=st_t[:, 8:9], scale=st_t[:, 7:8]),
                   reads=[stb], writes=[vb, PB[bk]])
            for hh in range(2):
                for b in (vb, wctb):
                    PE.wait(b.w)
                pb = PB[2 + hh]
                PE.wait(pb.w)
                for t in pb.r.values():
                    PE.wait(t)
                ins = None
                for h4 in range(4):
                    h = hh * 4 + h4
                    ins = nc.tensor.matmul(P[2 + hh][:, h4 * 128:(h4 + 1) * 128], vT[:, n % 2, h * 128:(h + 1) * 128],
                                           wct[:, h, :], start=True, stop=True)
                tok = PE.signal(ins)
                vb.r["pe"] = tok
                wctb.r["pe"] = tok
                pb.w = tok
                pb.r = {}
            lo = max(tk, so)
            if lo < tk + 128:
                w_ = tk + 128 - lo
                tm8 = tmpf[:, 0:2, :].rearrange("p a (h t) -> p (a h) t", h=4)
                for h in range(8):
                    op(DVE, lambda: nc.vector.scalar_tensor_tensor(
                        out=tm8[:, h, lo - tk:128], in0=P[2 + h // 4][:, (h % 4) * 128 + lo - tk:(h % 4) * 128 + 128],
                        scalar=sm(S_VG + i * 8 + h), in1=T1[:, h, lo - tk:128], op0=ALU.mult, op1=ALU.add),
                       reads=[PB[2 + h // 4], t1b, consts], writes=[tmpfb[h // 4]])
                op(DVE, lambda: nc.vector.tensor_tensor(out=cat[:, :, lo - XO:tk + 128 - XO], in0=tm8[:, :, lo - tk:128],
                                                        in1=cat[:, :, lo - XO:tk + 128 - XO], op=ALU.mult),
                   reads=[tmpfb[0], tmpfb[1]], writes=[(catb[h_], lo, tk + 128) for h_ in range(8)])
        for half in range(2):
            for u in uvs[half]:
                W.release(u)

        if stage <= 6:
            if dbg:
                for c in range(8):
                    op(DVE, lambda: nc.vector.tensor_copy(out=x_t[:, c, :], in_=cat[:, c, :]), reads=[(catb[c], 0, NT)], writes=[(xb[c], 0, NT)])
            return
        wout_part(1)

    def mixer_c(l):
        i = l // 2
        par = l % 2
        hs0 = MIX_H_START[l]
        so = MIX_OUT_START[l]
        nh = NT - hs0
        tl = tiles(hs0, NT)
        for ti_, (t0, T) in enumerate(tl):
            rms_rstd(xs, xb, t0, T, p_rstd[:, t0 - hs0:t0 - hs0 + T], poolb)
            if l + 1 < DEPTH:
                assert len(tl) == 3
                ada_chunks(l + 1, 5 * ti_, 5 * ti_ + 5, bank=7, evac=False)
        if l + 1 < DEPTH:
            ada_evac(l + 1, 0, NORM_FILL, 7)
        nhal = HALO - hs0
        tlo_ = tiles(so, NT)
        pk = 0

        def s1(fc):
            si = fc % 2
            sa_, h32, h16f = psets[si]
            h16 = h16f.bitcast(BF16)
            sbuf_ = psetb[si]
            op(DVE, lambda: nc.vector.tensor_tensor(out=sa_[:, :nh], in0=xs(fc, hs0, nh), in1=p_rstd[:, :nh], op=ALU.mult),
               reads=[(xb[fc], hs0, NT), poolb], writes=[sbuf_])
            op(ACT, lambda: nc.scalar.activation(out=h32[:, :nh], in_=sa_[:, :nh], func=AF.Identity,
                                                 bias=B1(l, fc), scale=A1(l, fc)),
               reads=[vecb[par], modb[par]], writes=[sbuf_])
            op(POOL, lambda: nc.gpsimd.tensor_tensor(out=h32[:, :nhal], in0=h32[:, :nhal], in1=hmask_t[:, :nhal], op=ALU.mult),
               reads=[consts], writes=[sbuf_])
            op(ACT, lambda: nc.scalar.activation(out=h16[:, :nh], in_=h32[:, :nh], func=AF.Identity),
               writes=[sbuf_])

        def s2(fc):
            nonlocal pk
            g = fc // 4
            w = POOLW[g]
            si = fc % 2
            sa_, h32, h16f = psets[si]
            h16 = h16f.bitcast(BF16)
            sbuf_ = psetb[si]
            for (t0, T) in tlo_:
                pi = 4 + pk % 2
                pk += 1
                b0 = t0 - hs0
                mm_group(pi, T, [(identb_t[:], h16[:, b0 - k:b0 - k + T]) for k in range(w)], [sbuf_, miscb])
                op(DVE, lambda: nc.vector.scalar_tensor_tensor(out=hb_t[:, fc, t0:t0 + T], in0=P[pi][:, :T], scalar=1.0 / w,
                                                               in1=h32[:, b0:b0 + T], op0=ALU.mult, op1=ALU.subtract),
                   reads=[PB[pi], sbuf_], writes=[hbb[fc]])
                if t0 <= HALO < t0 + T:
                    off = HALO - t0
                    assert off + 16 <= T
                    f0 = HALO - hs0
                    t16 = tmpf[:, 0, :16]
                    op(DVE, lambda: nc.vector.tensor_tensor(out=t16, in0=P[pi][:, off:off + 16],
                                                            in1=invc_t[:, g * 16:(g + 1) * 16], op=ALU.mult),
                       reads=[PB[pi], consts], writes=[tmpfb[0]])
                    op(DVE, lambda: nc.vector.tensor_tensor(out=hb_t[:, fc, HALO:HALO + 16], in0=t16,
                                                            in1=h32[:, f0:f0 + 16], op=ALU.subtract),
                       reads=[tmpfb[0], sbuf_], writes=[hbb[fc]])

        s1(0)
        for fc in range(16):
            if fc + 1 < 16:
                s1(fc + 1)
            s2(fc)
        tlo = tiles(so, NT)
        pc = 0
        for g in range(4):
            u = W.get(("pool", i, g))
            ua = slot_ap(u).rearrange("p (k m) -> p k m", k=4)
            for oc in range(4):
                f = g * 4 + oc
                for (t0, T) in tlo:
                    pi = (4, 5, 0, 1, 2, 3)[pc % 6]
                    pc += 1
                    mm_group(pi, T, [(ua[:, ic, oc * 128:(oc + 1) * 128], hs_(g * 4 + ic, t0, T)) for ic in range(4)],
                             [slot_buf(u)] + hbb[g * 4:g * 4 + 4])
                    x_update(f, t0, T, pi, G1(l, f), [vecb[par]])
            W.release(u)

    def src0(fc, t0, T):
        if t0 < XO:
            assert t0 + T <= XO
            return xh0[:, fc, t0:t0 + T]
        return xs(fc, t0, T)

    ada_chunks(0, 0, 32)
    if n_layers >= 1:
        rms_rstd(src0, [consts] * 16, 0, XO, rstd_all[:, 0:XO], rstdallb)
        for (t0_, T_) in tiles(XO, NT):
            rms_rstd(xs, xb, t0_, T_, rstd_all[:, t0_:t0_ + T_], rstdallb)
    ada_vectors(0, 'a')
    for l in range(n_layers):
        if l % 2 == 0:
            mixer_ab(l)
        else:
            mixer_c(l)
        if stage_in <= 7 and l == n_layers - 1:
            break
        ffn_phase(l)

    if dbg:
        dbgb = Buf("dbgst", nc, dma=True)
        for fc in range(16):
            dma_load(SP, dbgb, d_dbg[fc], x_t[:, fc, :], reads=[(xb[fc], 0, NT)])
        SP.wait(dbgb.w)

    for (t0, T) in tiles(HALO, NT):
        rstd = tmpf[:, 0, :T]
        rms_rstd(xs, xb, t0, T, rstd, tmpfb[0])
        for fc in range(16):
            op(DVE, lambda: nc.vector.tensor_tensor(out=xs(fc, t0, T), in0=xs(fc, t0, T), in1=rstd, op=ALU.mult),
               reads=[tmpfb[0]], writes=[(xb[fc], t0, t0 + T)])
            op(ACT, lambda: nc.scalar.activation(out=xs(fc, t0, T), in_=xs(fc, t0, T), func=AF.Identity,
                                                 scale=sm(S_FING + fc)),
               reads=[consts], writes=[(xb[fc], t0, t0 + T)])
        with nc.allow_non_contiguous_dma(reason="per-tile output store"):
            dma_load(SP, outb, d_out.rearrange("f p t -> p f t")[:, :, t0 - HALO:t0 - HALO + T],
                     x_t[:, :, t0 - XO:t0 - XO + T], reads=[(xb[fc_], t0, t0 + T) for fc_ in range(16)])
    SP.wait(outb.w)
    assert W.next == NU or n_layers < DEPTH, (W.next, NU)
    nc._used_units = W.next
    return nc, plan


def _pc(v, n):
    return np.ascontiguousarray(np.asarray(v, np.float32).reshape(n, 128).T)


def prep_inputs(inp, plan, cores=None):
    f32 = np.float32
    x = np.asarray(inp["x"], f32)[0]
    wstream = np.empty((len(plan), 128, UNIT), f32)
    for u, spec in enumerate(plan):
        fill_unit(spec, inp, wstream[u])
    sm = np.zeros((128, NS), f32)
    for l in range(DEPTH):
        sm[:, S_ADAB + l * 96:S_ADAB + (l + 1) * 96] = _pc(inp["ada_b"][l], 96)
        sm[:, S_GMIX + l * 16:S_GMIX + (l + 1) * 16] = _pc(inp["norm_mix_g"][l], 16)
        sm[:, S_GFFN + l * 16:S_GFFN + (l + 1) * 16] = _pc(inp["norm_ffn_g"][l], 16)
    sm[:, S_FING:S_FING + 16] = _pc(inp["final_g"], 16)
    for i in range(2):
        cwv = np.asarray(inp["a_conv_w"][i], f32)
        sm[:, S_CONVW + i * 8 * CONVW:S_CONVW + (i + 1) * 8 * CONVW] = \
            cwv.reshape(CONVW, 8, 128).transpose(2, 1, 0).reshape(128, 8 * CONVW)
        sm[:, S_CONVB + i * 8:S_CONVB + (i + 1) * 8] = _pc(inp["a_conv_b"][i], 8)
        sm[:, S_AG + i * 8:S_AG + (i + 1) * 8] = _pc(inp["a_norm_g"][i], 8)
        sm[:, S_AB + i * 8:S_AB + (i + 1) * 8] = _pc(inp["a_norm_b"][i], 8)
        sm[:, S_VG + i * 8:S_VG + (i + 1) * 8] = _pc(inp["b_norm_g"][i], 8)
        sm[:, S_VB + i * 8:S_VB + (i + 1) * 8] = _pc(inp["b_norm_b"][i], 8)
        sm[:, S_PSC + i * 16:S_PSC + (i + 1) * 16] = _pc(inp["pool_scale"][i], 16)
    sm[:, S_CVEC:S_CVEC + 16] = _pc(inp["c"][0], 16)
    ident = np.eye(128, dtype=f32)
    cmask = np.triu(np.ones((128, 128), f32))
    bbias = np.empty((2, 128, 1024), f32)
    wsT = np.empty((2, 128, 1024), f32)
    for i in range(2):
        bbias[i] = np.broadcast_to(np.asarray(inp["b_bias"][i], f32).reshape(1, 1024), (128, 1024))
        wsT[i] = np.asarray(inp["b_w_s"][i], f32).transpose(2, 0, 1).reshape(128, 1024)
    maps = []
    for k in (range(NCORE) if cores is None else cores):
        lo = k * OWN - HALO
        xt = np.zeros((NT, D), f32)
        if lo < 0:
            xt[-lo:] = x[0:lo + NT]
        else:
            xt[:] = x[lo:lo + NT]
        xT = np.ascontiguousarray(xt.T).reshape(16, 128, NT)
        hmask = np.full((128, 256), 0.0 if k == 0 else 1.0, f32)
        invc = np.empty((128, 64), f32)
        for g, w in enumerate(POOLW):
            for j in range(16):
                pos = k * OWN + j
                invc[:, g * 16 + j] = 1.0 / min(pos + 1, w)
        maps.append({"xT": xT, "wstream": wstream, "smalls": sm, "hmask": hmask, "invcnt": invc,
                     "ident": ident, "cmask": cmask, "bbias": bbias, "wsT": wsT})
    return maps


_CACHE = {}


def kernel(**inputs):
    inp = {k: np.asarray(v) for k, v in inputs.items()}
    if "prog" not in _CACHE:
        _CACHE["prog"] = build_program()
    nc, plan = _CACHE["prog"]
    maps = prep_inputs(inp, plan)
    res = run_bass_kernel_spmd(nc, maps, core_ids=list(range(NCORE)))
    outs = []
    for k in range(NCORE):
        o = np.asarray(res.results[k]["outT"], np.float32).reshape(D, OWN)
        outs.append(o.T)
    return np.ascontiguousarray(np.concatenate(outs, axis=0)[None].astype(np.float32))
```

```python
import numpy as np
import concourse.bass as bass
import concourse.mybir as mybir
from concourse.bass_utils import run_bass_kernel_spmd

F32 = mybir.dt.float32
BF16 = mybir.dt.bfloat16
AF = mybir.ActivationFunctionType
ALU = mybir.AluOpType
AX = mybir.AxisListType

D = 2048
KC = 16
NCORE = 8
OWN = 1024
HALO = 256
NT = OWN + HALO
XO = 112
NX = NT - XO
DFF = 5632
JC = DFF // 128
GS = 4
NG = JC // GS
DEPTH = 4
EPS = 1e-6
NSLOT = 8
WAIT_SLACK = 4
UNIT = 2048
CONVW = 31
POOLW = (2, 4, 8, 16)

MIX_H_START = [0, 112, 128, 240]
MIX_OUT_START = [112, 128, 240, 256]
A_BACK = 32

_off = 0
def _take(n):
    global _off
    o = _off
    _off += n
    return o
S_ADAB = _take(DEPTH * 96)
S_GMIX = _take(DEPTH * 16)
S_GFFN = _take(DEPTH * 16)
S_FING = _take(16)
S_CONVW = _take(2 * 8 * CONVW)
S_CONVB = _take(2 * 8)
S_AG = _take(2 * 8)
S_AB = _take(2 * 8)
S_VG = _take(2 * 8)
S_VB = _take(2 * 8)
S_PSC = _take(2 * 16)
S_CVEC = _take(16)
NS = _off


def tiles(s, e, maxt=512):
    n = -(-(e - s) // maxt)
    base = -(-(e - s) // n)
    base = -(-base // 8) * 8
    out = []
    t = s
    while t < e:
        T = min(base, e - t)
        out.append((t, T))
        t += T
    return out


NORM_FILL = 15


def ada_ffn_base(l):
    return NORM_FILL if l in (1, 2) else 0


def ada_split(step, first=0, nsteps=JC - GS):
    step -= GS
    if step < 0:
        return 0, 0
    n = 96 - first
    lo = first + -(-n * step // nsteps)
    hi = first + -(-n * (step + 1) // nsteps)
    return lo, hi


def unit_plan():
    plan = []
    for c in range(32):
        plan.append(("ada", 0, c))
    for l in range(DEPTH):
        i = l // 2
        if l % 2 == 0:
            if l == 2:
                for c in range(NORM_FILL):
                    plan.append(("ada", 3, c))
            for c in range(8):
                plan.append(("win", i, c))
                plan.append(("win", i, 8 + c))
                if l == 0:
                    for cc in range(32 + 8 * c, 40 + 8 * c):
                        plan.append(("ada", 0, cc))
            for fp in range(8):
                plan.append(("wout", i, 0, fp))
            for c in range(8):
                plan.append(("win", i, 16 + c))
            for half in range(2):
                for kg in range(4):
                    plan.append(("winv", i, half, kg))
            for fp in range(8):
                plan.append(("wout", i, 1, fp))
        else:
            if l + 1 < DEPTH:
                for c in range(NORM_FILL):
                    plan.append(("ada", l + 1, c))
            for g in range(4):
                plan.append(("pool", i, g))
        if l + 1 < DEPTH:
            for c in range(ada_ffn_base(l), ada_ffn_base(l) + NORM_FILL):
                plan.append(("ada", l + 1, c))
        for q in range(NG):
            for jj in range(GS):
                plan.append(("w1", l, q * GS + jj))
                plan.append(("w3", l, q * GS + jj))
                if l + 1 < DEPTH:
                    lo, hi = ada_split(q * GS + jj, ada_ffn_base(l) + NORM_FILL)
                    for c in range(lo, hi):
                        plan.append(("ada", l + 1, c))
            for jj in range(GS):
                plan.append(("w2", l, q * GS + jj))
    return plan


def _kmajor(a):
    return a.reshape(16, 128, 128).transpose(1, 0, 2).reshape(128, UNIT)


def fill_unit(spec, inp, out):
    kind = spec[0]
    if kind == "ada":
        _, l, c = spec
        out[...] = _kmajor(inp["ada_w"][l][:, c * 128:(c + 1) * 128])
    elif kind == "win":
        _, i, m = spec
        out[...] = _kmajor(inp["ab_w_in"][i][:, m * 128:(m + 1) * 128])
    elif kind == "winv":
        _, i, half, kg = spec
        w = inp["ab_w_in"][i][kg * 512:(kg + 1) * 512, 3072 + half * 512:3072 + (half + 1) * 512]
        out[...] = w.reshape(4, 128, 512).transpose(1, 0, 2).reshape(128, UNIT)
    elif kind == "wout":
        _, i, part, fp = spec
        w = inp["ab_w_out"][i][part * 1024:(part + 1) * 1024, fp * 256:(fp + 1) * 256]
        out[...] = w.reshape(8, 128, 2, 128).transpose(1, 2, 0, 3).reshape(128, UNIT)
    elif kind == "w1":
        _, l, j = spec
        out[...] = _kmajor(inp["ffn_w1"][l][:, j * 128:(j + 1) * 128])
    elif kind == "w3":
        _, l, j = spec
        out[...] = _kmajor(inp["ffn_w3"][l][:, j * 128:(j + 1) * 128])
    elif kind == "w2":
        _, l, j = spec
        out[...] = inp["ffn_w2"][l][j * 128:(j + 1) * 128, :]
    elif kind == "pool":
        _, i, g = spec
        out[...] = inp["pool_w"][i][g].reshape(4, 128, 512).transpose(1, 0, 2).reshape(128, UNIT)
    else:
        raise ValueError(spec)


class Tok:
    __slots__ = ("eng", "sem", "val", "key")

    def __init__(self, eng, sem, val, key):
        self.eng, self.sem, self.val, self.key = eng, sem, val, key


class Eng:
    def __init__(self, nc, raw, name):
        self.nc, self.raw, self.name = nc, raw, name
        self.sem = nc.alloc_semaphore("es_" + name)
        self.count = 0
        self.waited = {}

    def wait(self, tok):
        if tok is None or (tok.eng is self and self.name == "pe"):
            return
        if self.waited.get(tok.key, 0) >= tok.val:
            return
        val = tok.val
        if tok.eng is not None and tok.eng is not self:
            alt = tok.eng.count - WAIT_SLACK
            if alt > val:
                val = alt
        self.raw.wait_ge(tok.sem, val)
        self.waited[tok.key] = val

    def signal(self, ins):
        ins.then_inc(self.sem, 1)
        self.count += 1
        return Tok(self, self.sem, self.count, self.name)


class Buf:
    def __init__(self, name, nc=None, dma=False):
        self.name = name
        self.w = None
        self.r = {}
        if dma:
            self.dsem = nc.alloc_semaphore("ds_" + name)
            self.dcount = 0

    def deps(self, write, lo=None, hi=None):
        out = [self.w]
        if write:
            out.extend(self.r.values())
        return out

    def record(self, tok, write, ename, lo=None, hi=None):
        if write:
            self.w = tok
            self.r = {}
        else:
            self.r[ename] = tok


class RBuf:
    def __init__(self, name):
        self.name = name
        self.ent = []

    def deps(self, write, lo, hi):
        out = []
        for e in self.ent:
            if e[0] < hi and lo < e[1] and (write or e[3]):
                out.append(e[2])
        return out

    def record(self, tok, write, ename, lo, hi):
        if write:
            self.ent = [e for e in self.ent if not (lo <= e[0] and e[1] <= hi)]
        else:
            self.ent = [e for e in self.ent if not ((not e[3]) and e[4] == ename and lo <= e[0] and e[1] <= hi)]
        self.ent.append([lo, hi, tok, write, ename])

    def set_all(self, tok, lo, hi):
        self.ent = [[lo, hi, tok, True, "init"]]


def _items(lst):
    for it in lst:
        if isinstance(it, tuple):
            yield it
        else:
            yield (it, None, None)


def op(E, fn, reads=(), writes=()):
    rl = list(_items(reads))
    wl = list(_items(writes))
    for b, lo, hi in rl:
        for t in b.deps(False, lo, hi):
            E.wait(t)
    for b, lo, hi in wl:
        for t in b.deps(True, lo, hi):
            E.wait(t)
    ins = fn()
    tok = E.signal(ins)
    for b, lo, hi in rl:
        b.record(tok, False, E.name, lo, hi)
    for b, lo, hi in wl:
        b.record(tok, True, E.name, lo, hi)
    return ins


def dma_load(Q, buf, out_ap, in_ap, reads=()):
    rl = list(_items(reads))
    for b, lo, hi in rl:
        for t in b.deps(False, lo, hi):
            Q.wait(t)
    if buf.w is not None and buf.w.key != "d_" + buf.name:
        Q.wait(buf.w)
    for t in buf.r.values():
        Q.wait(t)
    ins = Q.raw.dma_start(out=out_ap, in_=in_ap)
    ins.then_inc(buf.dsem, 16)
    buf.dcount += 16
    tok = Tok(None, buf.dsem, buf.dcount, "d_" + buf.name)
    buf.w = tok
    buf.r = {}
    for b, lo, hi in rl:
        b.record(tok, False, "d_" + buf.name, lo, hi)
    return tok


def build_program(n_layers=DEPTH, dbg=False, plan_len=None, stage=99):
    stage_in = stage
    nc = bass.Bass("TRN2", target_bir_lowering=False)
    plan = unit_plan()[:plan_len]
    NU = len(plan)

    d_x = nc.dram_tensor("xT", [16, 128, NT], F32, kind="ExternalInput").ap()
    d_w = nc.dram_tensor("wstream", [NU, 128, UNIT], F32, kind="ExternalInput").ap()
    d_sm = nc.dram_tensor("smalls", [128, NS], F32, kind="ExternalInput").ap()
    d_hmask = nc.dram_tensor("hmask", [128, 256], F32, kind="ExternalInput").ap()
    d_invc = nc.dram_tensor("invcnt", [128, 64], F32, kind="ExternalInput").ap()
    d_ident = nc.dram_tensor("ident", [128, 128], F32, kind="ExternalInput").ap()
    d_cmask = nc.dram_tensor("cmask", [128, 128], F32, kind="ExternalInput").ap()
    d_bbias = nc.dram_tensor("bbias", [2, 128, 1024], F32, kind="ExternalInput").ap()
    d_wsT = nc.dram_tensor("wsT", [2, 128, 1024], F32, kind="ExternalInput").ap()
    d_out = nc.dram_tensor("outT", [16, 128, OWN], F32, kind="ExternalOutput").ap()
    if dbg:
        d_dbg = nc.dram_tensor("dbg", [16, 128, NX], F32, kind="ExternalOutput").ap()

    x_t = nc.alloc_sbuf_tensor("x_t", [128, 16, NX], F32)
    hb_t = nc.alloc_sbuf_tensor("hb_t", [128, 16, NT], BF16)
    ring_t = nc.alloc_sbuf_tensor("ring_t", [128, NSLOT, UNIT], BF16)
    SCR_BYTES = 52224
    scr_t = nc.alloc_sbuf_tensor("scr_t", [128, SCR_BYTES // 2], BF16)
    sm_t = nc.alloc_sbuf_tensor("sm_t", [128, NS], F32)
    hmask_t = nc.alloc_sbuf_tensor("hmask_t", [128, 256], F32)
    invc_t = nc.alloc_sbuf_tensor("invc_t", [128, 64], F32)
    ident_t = nc.alloc_sbuf_tensor("ident_t", [128, 128], F32)
    cmask_t = nc.alloc_sbuf_tensor("cmask_t", [128, 128], F32)
    ones_t = nc.alloc_sbuf_tensor("ones_t", [128, 128], BF16)
    identb_t = nc.alloc_sbuf_tensor("identb_t", [128, 128], BF16)
    cond_t = nc.alloc_sbuf_tensor("cond_t", [128, 16], BF16)
    mod_t = nc.alloc_sbuf_tensor("mod_t", [128, 2, 96], F32)
    vec_t = nc.alloc_sbuf_tensor("vec_t", [128, 2, 4, 16], F32)
    st_t = nc.alloc_sbuf_tensor("st_t", [128, 16], F32)
    eps_t = nc.alloc_sbuf_tensor("eps_t", [128, 1], F32)

    scr = scr_t[:]

    def carve(off_bytes, nbytes, dtype, shape):
        a = scr[:, off_bytes // 2:(off_bytes + nbytes) // 2]
        if dtype == F32:
            a = a.bitcast(F32)
        if shape is None:
            return a
        if len(shape) == 2:
            return a.rearrange("p (a b) -> p a b", a=shape[0])
        if len(shape) == 3:
            return a.rearrange("p (a b c) -> p a b c", a=shape[0], b=shape[1])
        return a

    o = 0
    CAT_B = 8 * NX * 2
    cat = carve(o, CAT_B, BF16, (8, NX)); o += CAT_B
    gbuf = carve(0, GS * NX * 2, BF16, (GS, NX))
    NTA = NT - (MIX_OUT_START[0] - A_BACK)
    apad = carve(o, 2 * NTA * 2, BF16, (2, NTA)); o += 2 * NTA * 2
    diag = carve(o, CONVW * 128 * 2, BF16, (CONVW, 128)); o += CONVW * 128 * 2
    T1 = carve(o, 4096, F32, (8, 128)); o += 4096
    wct = carve(o, 2048, BF16, (8, 128)); o += 2048
    vT = carve(o, 4096, BF16, (2, 1024)); o += 4096
    tmpf = carve(o, 4 * 2048, F32, (4, 512)); o += 4 * 2048
    tmpb = carve(o, 2 * 1024, BF16, (2, 512)); o += 2 * 1024
    assert o <= SCR_BYTES, o
    NH = NT - MIX_H_START[1]
    po = 0
    p_rstd = carve(po, NH * 4, F32, None); po += NH * 4
    psets = []
    for _si in range(2):
        st_ = []
        for _k in range(3):
            st_.append(carve(po, NH * 4, F32, None)); po += NH * 4
        psets.append(st_)
    assert po <= o - 4 * 2048 - 2 * 1024, (po, o)
    rstd_all = carve(9344, NT * 4, F32, None)
    xh0 = carve(0, 16 * XO * 4, F32, (16, XO))
    wstmp = carve(CAT_B, 4096, F32, (8, 128))

    P = [nc.alloc_psum_tensor(f"ps{i}", [128, 512], F32) for i in range(8)]
    PB = [Buf(f"ps{i}") for i in range(8)]

    PE = Eng(nc, nc.tensor, "pe")
    ACT = Eng(nc, nc.scalar, "act")
    DVE = Eng(nc, nc.vector, "dve")
    POOL = Eng(nc, nc.gpsimd, "pool")
    SP = Eng(nc, nc.sync, "sp")

    xb = [RBuf(f"x{f}") for f in range(16)]
    hbb = [Buf(f"hb{f}") for f in range(16)]
    hbt = [Buf(f"hbt{i}") for i in range(4)]
    slotb = [Buf(f"slot{s}", nc, dma=True) for s in range(NSLOT)]
    xload = Buf("xload", nc, dma=True)
    consts = Buf("consts", nc, dma=True)
    catb = [RBuf(f"cat{c}") for c in range(8)]
    apadb = [Buf("apad0"), Buf("apad1")]
    diagb = Buf("diag")
    diagb2 = Buf("diag2")
    t1b = Buf("T1", nc, dma=True)
    wctb = Buf("wct")
    wstb = Buf("wstmp", nc, dma=True)
    vTb = [Buf("vT0"), Buf("vT1")]
    tmpfb = [Buf(f"tmpf{i}") for i in range(4)]
    tmpbb = [Buf(f"tmpb{i}") for i in range(2)]
    modb = [Buf("mod0"), Buf("mod1")]
    vecb = [Buf("vec0"), Buf("vec1")]
    stb = Buf("st")
    condb = Buf("cond")
    miscb = Buf("misc")
    poolb = Buf("poolscr")
    rstdallb = Buf("rstdall")
    psetb = [Buf("pset0"), Buf("pset1")]
    outb = Buf("outst", nc, dma=True)

    class Ring:
        def __init__(self):
            self.issued = 0
            self.next = 0
            self.released = [False] * NU

        def pump(self):
            while self.issued < NU and (self.issued < NSLOT or self.released[self.issued - NSLOT]):
                u = self.issued
                s = u % NSLOT
                dma_load(POOL, slotb[s], ring_t[:, s, :], d_w[u])
                self.issued += 1

        def get(self, spec):
            u = self.next
            assert plan[u] == spec, (u, plan[u], spec)
            self.pump()
            assert self.issued > u, ("ring stall", u, spec)
            self.next += 1
            return u

        def release(self, u):
            self.released[u] = True
            self.pump()

    W = Ring()

    def slot_ap(u):
        return ring_t[:, u % NSLOT, :]

    def slot_buf(u):
        return slotb[u % NSLOT]

    def xs(fc, t0, T):
        return x_t[:, fc, t0 - XO:t0 - XO + T]

    def hs_(fc, t0, T):
        return hb_t[:, fc, t0:t0 + T]

    def sm(off, n=1):
        return sm_t[:, off:off + n]

    def mm_group(ps_i, T, pairs, reads, Mrows=128, col0=0, rec_only=()):
        rl = list(_items(reads))
        for b, lo, hi in rl:
            for t in b.deps(False, lo, hi):
                PE.wait(t)
        pb = PB[ps_i]
        for t in pb.deps(True):
            PE.wait(t)
        n = len(pairs)
        ins = None
        for idx, (l_ap, r_ap) in enumerate(pairs):
            ins = nc.tensor.matmul(P[ps_i][:Mrows, col0:col0 + T], l_ap, r_ap,
                                   start=(idx == 0), stop=(idx == n - 1))
        tok = PE.signal(ins)
        for b, lo, hi in rl:
            b.record(tok, False, "pe", lo, hi)
        for b, lo, hi in _items(rec_only):
            b.record(tok, False, "pe", lo, hi)
        pb.record(tok, True, "pe")
        return tok

    for fc in range(16):
        dma_load(SP, xload, x_t[:, fc, :], d_x[fc][:, XO:])
    for fc in range(16):
        dma_load(SP, consts, xh0[:, fc, :], d_x[fc][:, 0:XO])
    dma_load(SP, consts, sm_t[:], d_sm)
    dma_load(SP, consts, hmask_t[:], d_hmask)
    dma_load(SP, consts, invc_t[:], d_invc)
    dma_load(SP, consts, ident_t[:], d_ident)
    dma_load(SP, consts, cmask_t[:], d_cmask)
    for f in range(16):
        xb[f].set_all(xload.w, 0, NT)
    op(DVE, lambda: nc.vector.memset(ones_t[:], 1.0), writes=[miscb])
    op(DVE, lambda: nc.vector.memset(eps_t[:], EPS), writes=[miscb])
    op(DVE, lambda: nc.vector.tensor_copy(out=identb_t[:], in_=ident_t[:]), reads=[consts], writes=[miscb])
    op(ACT, lambda: nc.scalar.activation(out=cond_t[:], in_=sm(S_CVEC, 16), func=AF.Silu),
       reads=[consts], writes=[condb])
    W.pump()

    def ada_chunks(l, c_lo, c_hi, bank=7, evac=True):
        par = l % 2
        if c_hi <= c_lo:
            return
        for c in range(c_lo, c_hi):
            u = W.get(("ada", l, c))
            ua = slot_ap(u).rearrange("p (k m) -> p k m", k=16)
            pairs = [(ua[:, k, :], cond_t[:, k:k + 1]) for k in range(16)]
            for b in (slot_buf(u), condb):
                PE.wait(b.w)
            pb = PB[bank]
            if c == c_lo:
                PE.wait(pb.w)
                for t in pb.r.values():
                    PE.wait(t)
            ins = None
            for k in range(16):
                ins = nc.tensor.matmul(P[bank][:, c:c + 1], pairs[k][0], pairs[k][1],
                                       start=(k == 0), stop=(k == 15))
            tok = PE.signal(ins)
            slot_buf(u).r["pe"] = tok
            condb.r["pe"] = tok
            pb.w = tok
            pb.r = {}
            W.release(u)
        if evac:
            ada_evac(l, c_lo, c_hi, bank)

    def ada_evac(l, c_lo, c_hi, bank):
        par = l % 2
        op(DVE, lambda: nc.vector.tensor_tensor(out=mod_t[:, par, c_lo:c_hi], in0=P[bank][:, c_lo:c_hi],
                                                in1=sm(S_ADAB + l * 96 + c_lo, c_hi - c_lo), op=ALU.add),
           reads=[PB[bank], consts], writes=[modb[par]])

    def ada_vectors(l, part=None):
        par = l % 2
        i = l // 2
        if part == 'b':
            return ada_vectors_b(l)
        op(DVE, lambda: nc.vector.scalar_tensor_tensor(out=vec_t[:, par, 0, :], in0=mod_t[:, par, 16:32], scalar=1.0,
                                                       in1=sm(S_GMIX + l * 16, 16), op0=ALU.add, op1=ALU.mult),
           reads=[modb[par], consts], writes=[vecb[par]])
        if part == 'a':
            return
        ada_vectors_b(l)

    def ada_vectors_b(l):
        par = l % 2
        i = l // 2
        if l % 2 == 1:
            op(DVE, lambda: nc.vector.tensor_tensor(out=vec_t[:, par, 1, :], in0=mod_t[:, par, 32:48],
                                                    in1=sm(S_PSC + i * 16, 16), op=ALU.mult),
               reads=[modb[par], consts], writes=[vecb[par]])
        else:
            op(DVE, lambda: nc.vector.tensor_copy(out=vec_t[:, par, 1, :], in_=mod_t[:, par, 32:48]),
               reads=[modb[par]], writes=[vecb[par]])
        op(DVE, lambda: nc.vector.scalar_tensor_tensor(out=vec_t[:, par, 2, :], in0=mod_t[:, par, 64:80], scalar=1.0,
                                                       in1=sm(S_GFFN + l * 16, 16), op0=ALU.add, op1=ALU.mult),
           reads=[modb[par], consts], writes=[vecb[par]])

    def A1(l, f): return vec_t[:, l % 2, 0, f:f + 1]
    def B1(l, f): return mod_t[:, l % 2, f:f + 1]
    def G1(l, f): return vec_t[:, l % 2, 1, f:f + 1]
    def A2(l, f): return vec_t[:, l % 2, 2, f:f + 1]
    def B2(l, f): return mod_t[:, l % 2, 48 + f:48 + f + 1]
    def G2(l, f): return mod_t[:, l % 2, 80 + f:80 + f + 1]

    def rms_rstd(src, src_bufs, t0, T, dst_ap, dst_buf):
        for fc in range(16):
            sq = hs_(fc, t0, T)
            sqb = hbb[fc]
            s_ap = src(fc, t0, T)
            if fc % 2 == 0:
                op(ACT, lambda: nc.scalar.activation(out=sq, in_=s_ap, func=AF.Square),
                   reads=[(src_bufs[fc], t0, t0 + T)], writes=[sqb])
            else:
                op(DVE, lambda: nc.vector.tensor_tensor(out=sq, in0=s_ap, in1=s_ap, op=ALU.mult),
                   reads=[(src_bufs[fc], t0, t0 + T)], writes=[sqb])
        for fc in range(16):
            sq = hs_(fc, t0, T)
            sqb = hbb[fc]
            PE.wait(sqb.w)
            PE.wait(miscb.w)
            if fc == 0:
                PE.wait(PB[6].w)
                for t in PB[6].r.values():
                    PE.wait(t)
            ins = nc.tensor.matmul(P[6][:, :T], ones_t[:], sq, start=(fc == 0), stop=(fc == 15))
            if fc == 15:
                tok = PE.signal(ins)
                for b_ in hbb:
                    b_.r["pe"] = tok
                PB[6].w = tok
                PB[6].r = {}
        op(ACT, lambda: nc.scalar.activation(out=dst_ap, in_=P[6][:, :T], func=AF.Sqrt,
                                             bias=eps_t[:, 0:1], scale=1.0 / D),
           reads=[PB[6], miscb], writes=[dst_buf])
        op(DVE, lambda: nc.vector.reciprocal(out=dst_ap, in_=dst_ap), reads=[dst_buf], writes=[dst_buf])

    def norm_phase(tile_list, src, src_bufs, Afn, Bfn, extra_reads, pre=None, after_stats=None):
        for ti_, (t0, T) in enumerate(tile_list):
            if pre is None:
                rstd = tmpf[:, 0, :T]
                rstd_buf = tmpfb[0]
                rms_rstd(src, src_bufs, t0, T, rstd, rstd_buf)
                if after_stats is not None:
                    after_stats(ti_)
            else:
                rstd = pre[0][:, t0:t0 + T]
                rstd_buf = pre[1]
            for fc in range(16):
                tb = 1 + fc % 2
                tt = tmpf[:, tb, :T]
                s_ap = src(fc, t0, T)
                op(DVE, lambda: nc.vector.tensor_tensor(out=tt, in0=s_ap, in1=rstd, op=ALU.mult),
                   reads=[(src_bufs[fc], t0, t0 + T), rstd_buf], writes=[tmpfb[tb]])
                op(ACT, lambda: nc.scalar.activation(out=hs_(fc, t0, T), in_=tt, func=AF.Identity,
                                                     bias=Bfn(fc), scale=Afn(fc)),
                   reads=[tmpfb[tb]] + extra_reads, writes=[hbb[fc]])
            hbt[ti_].w = Tok(ACT, ACT.sem, ACT.count, ACT.name)
            hbt[ti_].r = {}

    def x_update(f, t0, T, ps_i, gate_ap, extra_reads):
        op(DVE, lambda: nc.vector.scalar_tensor_tensor(out=xs(f, t0, T), in0=P[ps_i][:, :T], scalar=gate_ap,
                                                       in1=xs(f, t0, T), op0=ALU.mult, op1=ALU.add),
           reads=[PB[ps_i]] + extra_reads, writes=[(xb[f], t0, t0 + T)])

    def ffn_phase(l):
        par = l % 2
        s = MIX_OUT_START[l]
        tl = tiles(s, NT)
        if l + 1 < DEPTH:
            assert len(tl) == 3
            ab0 = ada_ffn_base(l)
            norm_phase(tl, xs, xb, lambda f: A2(l, f), lambda f: B2(l, f), [vecb[par], modb[par]],
                       after_stats=lambda ti: ada_chunks(l + 1, ab0 + 5 * ti, ab0 + 5 * ti + 5, bank=7, evac=False))
            ada_evac(l + 1, ab0, ab0 + NORM_FILL, 7)
        else:
            norm_phase(tl, xs, xb, lambda f: A2(l, f), lambda f: B2(l, f), [vecb[par], modb[par]])
        gbufs = catb[:GS]
        ada_c = 0
        pp = 0
        def a_step(jj, u1, u3, t0, T, hreads, rec):
            nonlocal pp
            a1 = slot_ap(u1).rearrange("p (k m) -> p k m", k=16)
            a3 = slot_ap(u3).rearrange("p (k m) -> p k m", k=16)
            pa, pb_ = (0, 1) if pp % 2 == 0 else (2, 3)
            pp += 1
            mm_group(pa, T, [(a1[:, k, :], hs_(k, t0, T)) for k in range(16)], [slot_buf(u1)] + hreads, rec_only=rec)
            mm_group(pb_, T, [(a3[:, k, :], hs_(k, t0, T)) for k in range(16)], [slot_buf(u3)] + hreads, rec_only=rec)
            tb = 2 + (pp % 2)
            sg = tmpf[:, tb, :T]
            op(ACT, lambda: nc.scalar.activation(out=sg, in_=P[pa][:, :T], func=AF.Silu),
               reads=[PB[pa]], writes=[tmpfb[tb]])
            op(DVE, lambda: nc.vector.tensor_tensor(out=gbuf[:, jj, t0 - XO:t0 - XO + T], in0=sg,
                                                    in1=P[pb_][:, :T], op=ALU.mult),
               reads=[tmpfb[tb], PB[pb_]], writes=[(gbufs[jj], t0, t0 + T)])

        for q in range(NG):
            if q == 0:
                us0 = []
                for jj in range(GS):
                    us0.append((W.get(("w1", l, jj)), W.get(("w3", l, jj))))
                for ti, (t0, T) in enumerate(tl):
                    for jj in range(GS):
                        u1, u3 = us0[jj]
                        a_step(jj, u1, u3, t0, T, [hbt[ti]], hbb)
                        if ti == len(tl) - 1:
                            W.release(u1)
                            W.release(u3)
            else:
                for jj in range(GS):
                    j = q * GS + jj
                    u1 = W.get(("w1", l, j))
                    u3 = W.get(("w3", l, j))
                    for (t0, T) in tl:
                        a_step(jj, u1, u3, t0, T, hbb, ())
                    W.release(u1)
                    W.release(u3)
                    if l + 1 < DEPTH:
                        alo, ahi = ada_split(j, ada_ffn_base(l) + NORM_FILL)
                        ada_chunks(l + 1, alo, ahi, bank=6 + j % 2)
            us = [W.get(("w2", l, q * GS + jj)) for jj in range(GS)]
            pc = 0
            for f in range(16):
                for (t0, T) in tl:
                    pi = (4, 5, 0, 1, 2, 3)[pc % 6]
                    pc += 1
                    mm_group(pi, T, [(slot_ap(us[jj])[:, f * 128:(f + 1) * 128], gbuf[:, jj, t0 - XO:t0 - XO + T])
                                     for jj in range(GS)],
                             [slot_buf(u) for u in us] + [(gb_, t0, t0 + T) for gb_ in gbufs])
                    x_update(f, t0, T, pi, G2(l, f), [modb[par]])
            for u in us:
                W.release(u)
        if l + 1 < DEPTH:
            ada_vectors(l + 1)

    def mixer_ab(l):
        stage = stage_in if l == n_layers - 1 else 99
        i = l // 2
        par = l % 2
        hs0 = MIX_H_START[l]
        so = MIX_OUT_START[l]
        a0 = so - A_BACK
        nta = NT - a0
        cw = S_CONVW + i * 8 * CONVW

        if l == 0:
            norm_phase([(0, XO)], src0, [consts] * 16, lambda f: A1(l, f), lambda f: B1(l, f), [vecb[par], modb[par]],
                       pre=(rstd_all, rstdallb))
            norm_phase(tiles(XO, NT), xs, xb, lambda f: A1(l, f), lambda f: B1(l, f), [vecb[par], modb[par]],
                       pre=(rstd_all, rstdallb))
        else:
            ntl = tiles(hs0, NT)
            if l == 2 and DEPTH > 3:
                assert len(ntl) == 3
                norm_phase(ntl, xs, xb, lambda f: A1(l, f), lambda f: B1(l, f), [vecb[par], modb[par]],
                           after_stats=lambda ti: ada_chunks(3, 5 * ti, 5 * ti + 5, bank=7, evac=False))
                ada_evac(3, 0, NORM_FILL, 7)
            else:
                norm_phase(ntl, xs, xb, lambda f: A1(l, f), lambda f: B1(l, f), [vecb[par], modb[par]])

        if stage <= 1:
            return
        for e_ in (PE, ACT, DVE):
            if e_.count > 0:
                SP.wait(Tok(e_, e_.sem, e_.count, e_.name))
        dma_load(SP, wstb, wstmp.rearrange("p a b -> p (a b)"), d_wsT[i])
        dma_load(SP, t1b, T1.rearrange("p a b -> p (a b)"), d_bbias[i])
        for h in range(8):
            op(DVE, lambda: nc.vector.tensor_tensor(out=wct[:, h, :], in0=wstmp[:, h, :], in1=cmask_t[:], op=ALU.mult),
               reads=[wstb, consts], writes=[wctb])
        wflat = wct.rearrange("p a b -> p (a b)")
        mm_group(0, 512, [(ones_t[:], wflat[:, 0:512])], [wctb, miscb])
        mm_group(1, 512, [(ones_t[:], wflat[:, 512:1024])], [wctb, miscb])
        for h in range(8):
            pi = h // 4
            op(DVE, lambda: nc.vector.scalar_tensor_tensor(out=T1[:, h, :], in0=P[pi][:, (h % 4) * 128:(h % 4 + 1) * 128],
                                                           scalar=sm(S_VB + i * 8 + h), in1=T1[:, h, :],
                                                           op0=ALU.mult, op1=ALU.add),
               reads=[PB[pi], t1b, consts], writes=[t1b])

        tla = tiles(a0, NT)
        tlo = tiles(so, NT)
        pp = 0

        def conv(c):
            nonlocal pp
            for (t0, T) in tlo:
                pi = 4 + pp % 2
                pp += 1
                base = (t0 - so) + 2
                mm_group(pi, T, [(diag[:, j, :], apad[:, c % 2, base + j:base + j + T]) for j in range(CONVW)],
                         [diagb, diagb2, apadb[c % 2]])
                op(ACT, lambda: nc.scalar.activation(out=cat[:, c, t0 - XO:t0 - XO + T], in_=P[pi][:, :T],
                                                     func=AF.Identity, bias=sm(S_CONVB + i * 8 + c), scale=1.0),
                   reads=[PB[pi], consts], writes=[(catb[c], t0, t0 + T)])

        def build_diag(c):
            for j in range(CONVW):
                if j % 2 == 0:
                    op(ACT, lambda: nc.scalar.activation(out=diag[:, j, :], in_=ident_t[:], func=AF.Identity,
                                                         scale=sm(cw + c * CONVW + j)),
                       reads=[consts], writes=[diagb])
                else:
                    op(DVE, lambda: nc.vector.tensor_scalar(out=diag[:, j, :], in0=ident_t[:],
                                                            scalar1=sm(cw + c * CONVW + j), scalar2=None, op0=ALU.mult),
                       reads=[consts], writes=[diagb2])

        pq = 0
        for c in range(8):
            uv = W.get(("win", i, c))
            ug = W.get(("win", i, 8 + c))
            av = slot_ap(uv).rearrange("p (k m) -> p k m", k=16)
            ag = slot_ap(ug).rearrange("p (k m) -> p k m", k=16)
            for (t0, T) in tla:
                pa, pb_ = (0, 1) if pq % 2 == 0 else (2, 3)
                pq += 1
                mm_group(pa, T, [(av[:, k, :], hs_(k, t0, T)) for k in range(16)], [slot_buf(uv)] + hbb)
                mm_group(pb_, T, [(ag[:, k, :], hs_(k, t0, T)) for k in range(16)], [slot_buf(ug)] + hbb)
                tb = 2 + (pq % 2)
                sg = tmpf[:, tb, :T]
                op(ACT, lambda: nc.scalar.activation(out=sg, in_=P[pb_][:, :T], func=AF.Sigmoid),
                   reads=[PB[pb_]], writes=[tmpfb[tb]])
                op(DVE, lambda: nc.vector.tensor_tensor(out=apad[:, c % 2, t0 - a0:t0 - a0 + T], in0=sg,
                                                        in1=P[pa][:, :T], op=ALU.mult),
                   reads=[tmpfb[tb], PB[pa]], writes=[apadb[c % 2]])
            nh = HALO - a0
            op(DVE, lambda: nc.vector.tensor_tensor(out=apad[:, c % 2, 0:nh], in0=apad[:, c % 2, 0:nh],
                                                    in1=hmask_t[:, 0:nh], op=ALU.mult),
               reads=[consts], writes=[apadb[c % 2]])
            W.release(uv)
            W.release(ug)
            if l == 0:
                ada_chunks(0, 32 + 8 * c, 40 + 8 * c, bank=6 + c % 2)
            if c >= 1:
                conv(c - 1)
            build_diag(c)
        conv(7)
        if l == 0:
            ada_vectors(0, 'b')

        if stage <= 2:
            return
        for (t0, T) in tlo:
            for c in range(8):
                sq = tmpb[:, c % 2, :T]
                sqb = tmpbb[c % 2]
                c_ap = cat[:, c, t0 - XO:t0 - XO + T]
                op(DVE, lambda: nc.vector.tensor_tensor(out=sq, in0=c_ap, in1=c_ap, op=ALU.mult),
                   reads=[(catb[c], t0, t0 + T)], writes=[sqb])
                for b in (sqb, miscb):
                    PE.wait(b.w)
                for t in catb[c].deps(False, t0, t0 + T):
                    PE.wait(t)
                if c == 0:
                    for pi in (6, 7):
                        PE.wait(PB[pi].w)
                        for t in PB[pi].r.values():
                            PE.wait(t)
                nc.tensor.matmul(P[6][:, :T], ones_t[:], c_ap, start=(c == 0), stop=(c == 7))
                ins = nc.tensor.matmul(P[7][:, :T], ones_t[:], sq, start=(c == 0), stop=(c == 7))
                tok = PE.signal(ins)
                sqb.r["pe"] = tok
                catb[c].record(tok, False, "pe", t0, t0 + T)
                for pi in (6, 7):
                    PB[pi].w = tok
                    PB[pi].r = {}
            mu = tmpf[:, 0, :T]
            tB = tmpf[:, 1, :T]
            op(DVE, lambda: nc.vector.tensor_scalar(out=mu, in0=P[6][:, :T], scalar1=1.0 / 1024, scalar2=None, op0=ALU.mult),
               reads=[PB[6]], writes=[tmpfb[0]])
            op(DVE, lambda: nc.vector.tensor_tensor(out=tB, in0=mu, in1=mu, op=ALU.mult),
               reads=[tmpfb[0]], writes=[tmpfb[1]])
            op(DVE, lambda: nc.vector.scalar_tensor_tensor(out=tB, in0=P[7][:, :T], scalar=1.0 / 1024, in1=tB,
                                                           op0=ALU.mult, op1=ALU.subtract),
               reads=[PB[7]], writes=[tmpfb[1]])
            op(ACT, lambda: nc.scalar.activation(out=tB, in_=tB, func=AF.Sqrt, bias=eps_t[:, 0:1], scale=1.0),
               reads=[tmpfb[1], miscb], writes=[tmpfb[1]])
            op(DVE, lambda: nc.vector.reciprocal(out=tB, in_=tB), reads=[tmpfb[1]], writes=[tmpfb[1]])
            op(DVE, lambda: nc.vector.tensor_tensor(out=mu, in0=mu, in1=tB, op=ALU.mult),
               reads=[tmpfb[1]], writes=[tmpfb[0]])
            for c in range(8):
                tb = 2 + c % 2
                u_ap = tmpf[:, tb, :T]
                c_ap = cat[:, c, t0 - XO:t0 - XO + T]
                op(DVE, lambda: nc.vector.tensor_tensor(out=u_ap, in0=c_ap, in1=tB, op=ALU.mult),
                   reads=[(catb[c], t0, t0 + T), tmpfb[1]], writes=[tmpfb[tb]])
                op(DVE, lambda: nc.vector.tensor_tensor(out=u_ap, in0=u_ap, in1=mu, op=ALU.subtract),
                   reads=[tmpfb[0]], writes=[tmpfb[tb]])
                op(ACT, lambda: nc.scalar.activation(out=c_ap, in_=u_ap, func=AF.Silu,
                                                     bias=sm(S_AB + i * 8 + c), scale=sm(S_AG + i * 8 + c)),
                   reads=[tmpfb[tb], consts], writes=[(catb[c], t0, t0 + T)])

        if stage <= 3:
            return
        def wout_part(part):
            pc = 0
            for fp in range(8):
                u = W.get(("wout", i, part, fp))
                ua = slot_ap(u).rearrange("p (f c m) -> p f c m", f=2, c=8)
                for ff in range(2):
                    f = 2 * fp + ff
                    for (t0, T) in tlo:
                        pi = (4, 5, 0, 1, 2, 3)[pc % 6]
                        pc += 1
                        mm_group(pi, T, [(ua[:, ff, c, :], cat[:, c, t0 - XO:t0 - XO + T]) for c in range(8)],
                                 [slot_buf(u)] + [(cb_, t0, t0 + T) for cb_ in catb])
                        x_update(f, t0, T, pi, G1(l, f), [vecb[par]])
                W.release(u)

        wout_part(0)
        if stage <= 4:
            return

        pq = 0
        for c in range(8):
            u = W.get(("win", i, 16 + c))
            ua = slot_ap(u).rearrange("p (k m) -> p k m", k=16)
            for (t0, T) in tlo:
                pa = pq % 2
                pq += 1
                mm_group(pa, T, [(ua[:, k, :], hs_(k, t0, T)) for k in range(16)], [slot_buf(u)] + hbb)
                op(ACT, lambda: nc.scalar.activation(out=cat[:, c, t0 - XO:t0 - XO + T], in_=P[pa][:, :T], func=AF.Identity),
                   reads=[PB[pa]], writes=[(catb[c], t0, t0 + T)])
            W.release(u)

        if stage <= 5:
            return
        uvs = [[W.get(("winv", i, half, kg)) for kg in range(4)] for half in range(2)]
        n0 = hs0 // 128
        NCH = NT // 128

        def bank_of(n, half):
            return (0, 1)[half] if (n - n0) % 2 == 0 else (4, 5)[half]

        def bv_proj(n):
            tk = n * 128
            for half in range(2):
                pairs = []
                for k in range(16):
                    ua = slot_ap(uvs[half][k // 4]).rearrange("p (k m) -> p k m", k=4)
                    pairs.append((hb_t[:, k, tk:tk + 128], ua[:, k % 4, :]))
                mm_group(bank_of(n, half), 512, pairs, [slot_buf(u) for u in uvs[half]] + hbb)

        bv_proj(n0)
        for n in range(n0, NCH):
            tk = n * 128
            if n + 1 < NCH:
                bv_proj(n + 1)
            op(DVE, lambda: nc.vector.memset(st_t[:, 0:4], 0.0), writes=[stb])
            for half in range(2):
                bk = bank_of(n, half)
                junk = tmpf[:, 2 + half, :]
                op(ACT, lambda: nc.scalar.activation(out=junk, in_=P[bk][:, :], func=AF.Identity,
                                                     accum_out=st_t[:, half:half + 1]),
                   writes=[PB[bk], tmpfb[2 + half], stb])
                op(ACT, lambda: nc.scalar.activation(out=junk, in_=P[bk][:, :], func=AF.Square,
                                                     accum_out=st_t[:, 2 + half:3 + half]),
                   writes=[PB[bk], tmpfb[2 + half], stb])
            op(DVE, lambda: nc.vector.tensor_tensor(out=st_t[:, 4:5], in0=st_t[:, 0:1], in1=st_t[:, 1:2], op=ALU.add), writes=[stb])
            op(DVE, lambda: nc.vector.tensor_tensor(out=st_t[:, 5:6], in0=st_t[:, 2:3], in1=st_t[:, 3:4], op=ALU.add), writes=[stb])
            op(DVE, lambda: nc.vector.tensor_scalar(out=st_t[:, 4:6], in0=st_t[:, 4:6], scalar1=1.0 / 1024, scalar2=None, op0=ALU.mult), writes=[stb])
            op(DVE, lambda: nc.vector.tensor_tensor(out=st_t[:, 6:7], in0=st_t[:, 4:5], in1=st_t[:, 4:5], op=ALU.mult), writes=[stb])
            op(DVE, lambda: nc.vector.tensor_tensor(out=st_t[:, 6:7], in0=st_t[:, 5:6], in1=st_t[:, 6:7], op=ALU.subtract), writes=[stb])
            op(ACT, lambda: nc.scalar.activation(out=st_t[:, 7:8], in_=st_t[:, 6:7], func=AF.Sqrt, bias=eps_t[:, 0:1], scale=1.0),
               reads=[stb, miscb], writes=[stb])
            op(DVE, lambda: nc.vector.reciprocal(out=st_t[:, 7:8], in_=st_t[:, 7:8]), reads=[stb], writes=[stb])
            op(DVE, lambda: nc.vector.scalar_tensor_tensor(out=st_t[:, 8:9], in0=st_t[:, 4:5], scalar=-1.0, in1=st_t[:, 7:8],
                                                           op0=ALU.mult, op1=ALU.mult), writes=[stb])
            vb = vTb[n % 2]
            for half in range(2):
                bk = bank_of(n, half)
                op(ACT, lambda: nc.scalar.activation(out=vT[:, n % 2, half * 512:(half + 1) * 512], in_=P[bk][:, :],
                                                     func=AF.Identity, bias=st_t[:, 8:9], scale=st_t[:, 7:8]),
                   reads=[stb], writes=[vb, PB[bk]])
            for hh in range(2):
                for b in (vb, wctb):
                    PE.wait(b.w)
                pb = PB[2 + hh]
                PE.wait(pb.w)
                for t in pb.r.values():
                    PE.wait(t)
                ins = None
                for h4 in range(4):
                    h = hh * 4 + h4
                    ins = nc.tensor.matmul(P[2 + hh][:, h4 * 128:(h4 + 1) * 128], vT[:, n % 2, h * 128:(h + 1) * 128],
                                           wct[:, h, :], start=True, stop=True)
                tok = PE.signal(ins)
                vb.r["pe"] = tok
                wctb.r["pe"] = tok
                pb.w = tok
                pb.r = {}
            lo = max(tk, so)
            if lo < tk + 128:
                w_ = tk + 128 - lo
                tm8 = tmpf[:, 0:2, :].rearrange("p a (h t) -> p (a h) t", h=4)
                for h in range(8):
                    op(DVE, lambda: nc.vector.scalar_tensor_tensor(
                        out=tm8[:, h, lo - tk:128], in0=P[2 + h // 4][:, (h % 4) * 128 + lo - tk:(h % 4) * 128 + 128],
                        scalar=sm(S_VG + i * 8 + h), in1=T1[:, h, lo - tk:128], op0=ALU.mult, op1=ALU.add),
                       reads=[PB[2 + h // 4], t1b, consts], writes=[tmpfb[h // 4]])
                op(DVE, lambda: nc.vector.tensor_tensor(out=cat[:, :, lo - XO:tk + 128 - XO], in0=tm8[:, :, lo - tk:128],
                                                        in1=cat[:, :, lo - XO:tk + 128 - XO], op=ALU.mult),
                   reads=[tmpfb[0], tmpfb[1]], writes=[(catb[h_], lo, tk + 128) for h_ in range(8)])
        for half in range(2):
            for u in uvs[half]:
                W.release(u)

        if stage <= 6:
            if dbg:
                for c in range(8):
                    op(DVE, lambda: nc.vector.tensor_copy(out=x_t[:, c, :], in_=cat[:, c, :]), reads=[(catb[c], 0, NT)], writes=[(xb[c], 0, NT)])
            return
        wout_part(1)

    def mixer_c(l):
        i = l // 2
        par = l % 2
        hs0 = MIX_H_START[l]
        so = MIX_OUT_START[l]
        nh = NT - hs0
        tl = tiles(hs0, NT)
        for ti_, (t0, T) in enumerate(tl):
            rms_rstd(xs, xb, t0, T, p_rstd[:, t0 - hs0:t0 - hs0 + T], poolb)
            if l + 1 < DEPTH:
                assert len(tl) == 3
                ada_chunks(l + 1, 5 * ti_, 5 * ti_ + 5, bank=7, evac=False)
        if l + 1 < DEPTH:
            ada_evac(l + 1, 0, NORM_FILL, 7)
        nhal = HALO - hs0
        tlo_ = tiles(so, NT)
        pk = 0

        def s1(fc):
            si = fc % 2
            sa_, h32, h16f = psets[si]
            h16 = h16f.bitcast(BF16)
            sbuf_ = psetb[si]
            op(DVE, lambda: nc.vector.tensor_tensor(out=sa_[:, :nh], in0=xs(fc, hs0, nh), in1=p_rstd[:, :nh], op=ALU.mult),
               reads=[(xb[fc], hs0, NT), poolb], writes=[sbuf_])
            op(ACT, lambda: nc.scalar.activation(out=h32[:, :nh], in_=sa_[:, :nh], func=AF.Identity,
                                                 bias=B1(l, fc), scale=A1(l, fc)),
               reads=[vecb[par], modb[par]], writes=[sbuf_])
            op(POOL, lambda: nc.gpsimd.tensor_tensor(out=h32[:, :nhal], in0=h32[:, :nhal], in1=hmask_t[:, :nhal], op=ALU.mult),
               reads=[consts], writes=[sbuf_])
            op(ACT, lambda: nc.scalar.activation(out=h16[:, :nh], in_=h32[:, :nh], func=AF.Identity),
               writes=[sbuf_])

        def s2(fc):
            nonlocal pk
            g = fc // 4
            w = POOLW[g]
            si = fc % 2
            sa_, h32, h16f = psets[si]
            h16 = h16f.bitcast(BF16)
            sbuf_ = psetb[si]
            for (t0, T) in tlo_:
                pi = 4 + pk % 2
                pk += 1
                b0 = t0 - hs0
                mm_group(pi, T, [(identb_t[:], h16[:, b0 - k:b0 - k + T]) for k in range(w)], [sbuf_, miscb])
                op(DVE, lambda: nc.vector.scalar_tensor_tensor(out=hb_t[:, fc, t0:t0 + T], in0=P[pi][:, :T], scalar=1.0 / w,
                                                               in1=h32[:, b0:b0 + T], op0=ALU.mult, op1=ALU.subtract),
                   reads=[PB[pi], sbuf_], writes=[hbb[fc]])
                if t0 <= HALO < t0 + T:
                    off = HALO - t0
                    assert off + 16 <= T
                    f0 = HALO - hs0
                    t16 = tmpf[:, 0, :16]
                    op(DVE, lambda: nc.vector.tensor_tensor(out=t16, in0=P[pi][:, off:off + 16],
                                                            in1=invc_t[:, g * 16:(g + 1) * 16], op=ALU.mult),
                       reads=[PB[pi], consts], writes=[tmpfb[0]])
                    op(DVE, lambda: nc.vector.tensor_tensor(out=hb_t[:, fc, HALO:HALO + 16], in0=t16,
                                                            in1=h32[:, f0:f0 + 16], op=ALU.subtract),
                       reads=[tmpfb[0], sbuf_], writes=[hbb[fc]])

        s1(0)
        for fc in range(16):
            if fc + 1 < 16:
                s1(fc + 1)
            s2(fc)
        tlo = tiles(so, NT)
        pc = 0
        for g in range(4):
            u = W.get(("pool", i, g))
            ua = slot_ap(u).rearrange("p (k m) -> p k m", k=4)
            for oc in range(4):
                f = g * 4 + oc
                for (t0, T) in tlo:
                    pi = (4, 5, 0, 1, 2, 3)[pc % 6]
                    pc += 1
                    mm_group(pi, T, [(ua[:, ic, oc * 128:(oc + 1) * 128], hs_(g * 4 + ic, t0, T)) for ic in range(4)],
                             [slot_buf(u)] + hbb[g * 4:g * 4 + 4])
                    x_update(f, t0, T, pi, G1(l, f), [vecb[par]])
            W.release(u)

    def src0(fc, t0, T):
        if t0 < XO:
            assert t0 + T <= XO
            return xh0[:, fc, t0:t0 + T]
        return xs(fc, t0, T)

    ada_chunks(0, 0, 32)
    if n_layers >= 1:
        rms_rstd(src0, [consts] * 16, 0, XO, rstd_all[:, 0:XO], rstdallb)
        for (t0_, T_) in tiles(XO, NT):
            rms_rstd(xs, xb, t0_, T_, rstd_all[:, t0_:t0_ + T_], rstdallb)
    ada_vectors(0, 'a')
    for l in range(n_layers):
        if l % 2 == 0:
            mixer_ab(l)
        else:
            mixer_c(l)
        if stage_in <= 7 and l == n_layers - 1:
            break
        ffn_phase(l)

    if dbg:
        dbgb = Buf("dbgst", nc, dma=True)
        for fc in range(16):
            dma_load(SP, dbgb, d_dbg[fc], x_t[:, fc, :], reads=[(xb[fc], 0, NT)])
        SP.wait(dbgb.w)

    for (t0, T) in tiles(HALO, NT):
        rstd = tmpf[:, 0, :T]
        rms_rstd(xs, xb, t0, T, rstd, tmpfb[0])
        for fc in range(16):
            op(DVE, lambda: nc.vector.tensor_tensor(out=xs(fc, t0, T), in0=xs(fc, t0, T), in1=rstd, op=ALU.mult),
               reads=[tmpfb[0]], writes=[(xb[fc], t0, t0 + T)])
            op(ACT, lambda: nc.scalar.activation(out=xs(fc, t0, T), in_=xs(fc, t0, T), func=AF.Identity,
                                                 scale=sm(S_FING + fc)),
               reads=[consts], writes=[(xb[fc], t0, t0 + T)])
        with nc.allow_non_contiguous_dma(reason="per-tile output store"):
            dma_load(SP, outb, d_out.rearrange("f p t -> p f t")[:, :, t0 - HALO:t0 - HALO + T],
                     x_t[:, :, t0 - XO:t0 - XO + T], reads=[(xb[fc_], t0, t0 + T) for fc_ in range(16)])
    SP.wait(outb.w)
    assert W.next == NU or n_layers < DEPTH, (W.next, NU)
    nc._used_units = W.next
    return nc, plan


def _pc(v, n):
    return np.ascontiguousarray(np.asarray(v, np.float32).reshape(n, 128).T)


def prep_inputs(inp, plan, cores=None):
    f32 = np.float32
    x = np.asarray(inp["x"], f32)[0]
    wstream = np.empty((len(plan), 128, UNIT), f32)
    for u, spec in enumerate(plan):
        fill_unit(spec, inp, wstream[u])
    sm = np.zeros((128, NS), f32)
    for l in range(DEPTH):
        sm[:, S_ADAB + l * 96:S_ADAB + (l + 1) * 96] = _pc(inp["ada_b"][l], 96)
        sm[:, S_GMIX + l * 16:S_GMIX + (l + 1) * 16] = _pc(inp["norm_mix_g"][l], 16)
        sm[:, S_GFFN + l * 16:S_GFFN + (l + 1) * 16] = _pc(inp["norm_ffn_g"][l], 16)
    sm[:, S_FING:S_FING + 16] = _pc(inp["final_g"], 16)
    for i in range(2):
        cwv = np.asarray(inp["a_conv_w"][i], f32)
        sm[:, S_CONVW + i * 8 * CONVW:S_CONVW + (i + 1) * 8 * CONVW] = \
            cwv.reshape(CONVW, 8, 128).transpose(2, 1, 0).reshape(128, 8 * CONVW)
        sm[:, S_CONVB + i * 8:S_CONVB + (i + 1) * 8] = _pc(inp["a_conv_b"][i], 8)
        sm[:, S_AG + i * 8:S_AG + (i + 1) * 8] = _pc(inp["a_norm_g"][i], 8)
        sm[:, S_AB + i * 8:S_AB + (i + 1) * 8] = _pc(inp["a_norm_b"][i], 8)
        sm[:, S_VG + i * 8:S_VG + (i + 1) * 8] = _pc(inp["b_norm_g"][i], 8)
        sm[:, S_VB + i * 8:S_VB + (i + 1) * 8] = _pc(inp["b_norm_b"][i], 8)
        sm[:, S_PSC + i * 16:S_PSC + (i + 1) * 16] = _pc(inp["pool_scale"][i], 16)
    sm[:, S_CVEC:S_CVEC + 16] = _pc(inp["c"][0], 16)
    ident = np.eye(128, dtype=f32)
    cmask = np.triu(np.ones((128, 128), f32))
    bbias = np.empty((2, 128, 1024), f32)
    wsT = np.empty((2, 128, 1024), f32)
    for i in range(2):
        bbias[i] = np.broadcast_to(np.asarray(inp["b_bias"][i], f32).reshape(1, 1024), (128, 1024))
        wsT[i] = np.asarray(inp["b_w_s"][i], f32).transpose(2, 0, 1).reshape(128, 1024)
    maps = []
    for k in (range(NCORE) if cores is None else cores):
        lo = k * OWN - HALO
        xt = np.zeros((NT, D), f32)
        if lo < 0:
            xt[-lo:] = x[0:lo + NT]
        else:
            xt[:] = x[lo:lo + NT]
        xT = np.ascontiguousarray(xt.T).reshape(16, 128, NT)
        hmask = np.full((128, 256), 0.0 if k == 0 else 1.0, f32)
        invc = np.empty((128, 64), f32)
        for g, w in enumerate(POOLW):
            for j in range(16):
                pos = k * OWN + j
                invc[:, g * 16 + j] = 1.0 / min(pos + 1, w)
        maps.append({"xT": xT, "wstream": wstream, "smalls": sm, "hmask": hmask, "invcnt": invc,
                     "ident": ident, "cmask": cmask, "bbias": bbias, "wsT": wsT})
    return maps


_CACHE = {}


def kernel(**inputs):
    inp = {k: np.asarray(v) for k, v in inputs.items()}
    if "prog" not in _CACHE:
        _CACHE["prog"] = build_program()
    nc, plan = _CACHE["prog"]
    maps = prep_inputs(inp, plan)
    res = run_bass_kernel_spmd(nc, maps, core_ids=list(range(NCORE)))
    outs = []
    for k in range(NCORE):
        o = np.asarray(res.results[k]["outT"], np.float32).reshape(D, OWN)
        outs.append(o.T)
    return np.ascontiguousarray(np.concatenate(outs, axis=0)[None].astype(np.float32))
```
